# Optimizing a Trainium2 kernel written in Bass

```python
import math
import jax
import jax.numpy as jnp
from jax import lax
import numpy as np

D_MODEL = 4096
BATCH = 8
SEQ = 2048
DEPTH = 4

CTX_LEN = 256
GRID_W = 64

BRANCH_WIDTH = 3 * D_MODEL // 8
N_BRANCH = 3
RG_WIDTH = BRANCH_WIDTH
RG_BLOCK = 128
RG_BLOCKS = RG_WIDTH // RG_BLOCK
RG_C = 8.0
CONV_W = 4
CONV_LEFT = (CONV_W - 1) // 2
DA_HEAD_DIM = 128
DA_HEADS = BRANCH_WIDTH // (2 * DA_HEAD_DIM)
DA_WIDTH = BRANCH_WIDTH
ROPE_BASE = 10000.0
Q_BLOCK = 128
POOL_WINDOWS = (2, 4, 8, 16)
POOL_GROUPS = len(POOL_WINDOWS)
POOL_WIDTH = BRANCH_WIDTH
POOL_GROUP_DIM = POOL_WIDTH // POOL_GROUPS
OFF_A_X = 0
OFF_A_GATE = OFF_A_X + RG_WIDTH
OFF_QKV = OFF_A_GATE + RG_WIDTH
OFF_POOL = OFF_QKV + 3 * DA_WIDTH
OFF_GATES = OFF_POOL + POOL_WIDTH
IN_COLS = OFF_GATES + N_BRANCH * D_MODEL
MOD_RANK = 256
N_MOD = 6
N_EXPERTS = 16
EXPERT_FF = 384
EC_FACTOR = 2
DEEPNORM_ALPHA = (2 * DEPTH) ** 0.25
DEEPNORM_BETA = (8 * DEPTH) ** -0.25
EPS = 1e-5

kernel_name = 'hybrid_rglru_diffattn_pool_ecmoe_dit'


def _layer_norm(v, g, b):
    vf = v.astype(jnp.float32)
    mu = jnp.mean(vf, axis=-1, keepdims=True)
    var = jnp.mean(jnp.square(vf - mu), axis=-1, keepdims=True)
    return ((vf - mu) * lax.rsqrt(var + EPS)).astype(v.dtype) * g + b


def _rms_norm(v, g):
    vf = v.astype(jnp.float32)
    return (vf * lax.rsqrt(jnp.mean(jnp.square(vf), axis=-1, keepdims=True) + EPS)).astype(v.dtype) * g


def _modulate(v, shift, scale):
    return v * (1.0 + scale) + shift


def _conv_centred(v, w, b):
    L = v.shape[1]
    vp = jnp.pad(v, ((0, 0), (CONV_LEFT, CONV_W - 1 - CONV_LEFT), (0, 0)))
    out = b
    for k in range(CONV_W):
        out = out + w[k] * vp[:, k:k + L]
    return out


def _block_diag(v, w, b):
    vb = v.reshape(v.shape[0], v.shape[1], RG_BLOCKS, RG_BLOCK)
    return jnp.einsum('blhi,hij->blhj', vb, w).reshape(v.shape) + b


def _rglru_coeffs(xc, wa, ba, wx, bx, lam):
    r = jax.nn.sigmoid(_block_diag(xc, wa, ba))
    i = jax.nn.sigmoid(_block_diag(xc, wx, bx))
    log_a = -RG_C * r * jax.nn.softplus(-lam)
    a = jnp.exp(log_a)
    mult = jnp.sqrt(-jnp.expm1(2.0 * log_a))
    return a, mult * (i * xc)


def _combine(e1, e2):
    a1, b1 = e1
    a2, b2 = e2
    return a1 * a2, a2 * b1 + b2


def _linear_scan(a, b, h0, reverse):
    a_cum, b_cum = lax.associative_scan(_combine, (a, b), axis=1, reverse=reverse)
    return a_cum * h0[:, None] + b_cum


def _rglru_mixer(xa_ctx, ga_ctx, xa_lat, ga_lat, conv_w, conv_b, wa, ba, wx, bx, lam):
    xc_ctx = _conv_centred(xa_ctx, conv_w, conv_b)
    xc_lat = _conv_centred(xa_lat, conv_w, conv_b)
    h_ctx = None
    h_lat = None
    for d, rev in enumerate((False, True)):
        a_c, b_c = _rglru_coeffs(xc_ctx, wa[d], ba[d], wx[d], bx[d], lam[d])
        hc = _linear_scan(a_c, b_c, jnp.zeros_like(b_c[:, 0]), rev)
        final = hc[:, 0] if rev else hc[:, -1]
        a_l, b_l = _rglru_coeffs(xc_lat, wa[d], ba[d], wx[d], bx[d], lam[d])
        hl = _linear_scan(a_l, b_l, final, rev)
        h_ctx = hc if h_ctx is None else h_ctx + hc
        h_lat = hl if h_lat is None else h_lat + hl
    return jax.nn.gelu(ga_ctx) * h_ctx, jax.nn.gelu(ga_lat) * h_lat


def _rope_2d(t, rows, cols):
    half = DA_HEAD_DIM // 2
    quarter = half // 2
    inv_freq = ROPE_BASE ** (-jnp.arange(quarter, dtype=jnp.float32) / quarter)

    def rot(v, pos):
        ang = pos.astype(jnp.float32)[:, None] * inv_freq
        cos = jnp.cos(ang)[None, :, None, None, :].astype(v.dtype)
        sin = jnp.sin(ang)[None, :, None, None, :].astype(v.dtype)
        v1, v2 = v[..., :quarter], v[..., quarter:]
        return jnp.concatenate([v1 * cos - v2 * sin, v1 * sin + v2 * cos], axis=-1)

    return jnp.concatenate([rot(t[..., :half], rows), rot(t[..., half:], cols)], axis=-1)


def _diff_attend(q, k, v, lam):
    s = jnp.einsum('bqhcd,bkhcd->bhcqk', q, k).astype(jnp.float32) * (DA_HEAD_DIM ** -0.5)
    p = jax.nn.softmax(s, axis=-1)
    p_diff = p[:, :, 0] - lam * p[:, :, 1]
    return jnp.einsum('bhqk,bkhe->bqhe', p_diff.astype(v.dtype), v)


def _split_qkv(p):
    b, L, _ = p.shape
    q = p[..., :DA_WIDTH].reshape(b, L, DA_HEADS, 2, DA_HEAD_DIM)
    k = p[..., DA_WIDTH:2 * DA_WIDTH].reshape(b, L, DA_HEADS, 2, DA_HEAD_DIM)
    v = p[..., 2 * DA_WIDTH:].reshape(b, L, DA_HEADS, 2 * DA_HEAD_DIM)
    return q, k, v


def _diff_attention_mixer(qkv_ctx, qkv_lat, rows, cols, lq1, lk1, lq2, lk2, norm_g, lambda_init):
    lam = (jnp.exp(jnp.sum(lq1.astype(jnp.float32) * lk1.astype(jnp.float32)))
           - jnp.exp(jnp.sum(lq2.astype(jnp.float32) * lk2.astype(jnp.float32))) + lambda_init)
    q_c, k_c, v_c = _split_qkv(qkv_ctx)
    q_l, k_l, v_l = _split_qkv(qkv_lat)
    q_l = _rope_2d(q_l, rows, cols)
    k_l = _rope_2d(k_l, rows, cols)
    o_c = _diff_attend(q_c, k_c, v_c, lam)
    k_all = jnp.concatenate([k_c, k_l], axis=1)
    v_all = jnp.concatenate([v_c, v_l], axis=1)
    b, n = q_l.shape[:2]
    q_blocks = jnp.swapaxes(q_l.reshape(b, n // Q_BLOCK, Q_BLOCK, DA_HEADS, 2, DA_HEAD_DIM), 0, 1)
    o_l = lax.map(lambda qb: _diff_attend(qb, k_all, v_all, lam), q_blocks)
    o_l = jnp.swapaxes(o_l, 0, 1).reshape(b, n, DA_HEADS, 2 * DA_HEAD_DIM)
    o_c = (_rms_norm(o_c, norm_g) * (1.0 - lambda_init)).reshape(b, -1, DA_WIDTH)
    o_l = (_rms_norm(o_l, norm_g) * (1.0 - lambda_init)).reshape(b, n, DA_WIDTH)
    return o_c, o_l


def _pool_mixer(v, w, scale):
    b, L, _ = v.shape
    cs = jnp.pad(jnp.cumsum(v.astype(jnp.float32), axis=1), ((0, 0), (1, 0), (0, 0)))
    t = jnp.arange(L)
    outs = []
    for g, win in enumerate(POOL_WINDOWS):
        lo = jnp.clip(t - win // 2, 0, L)
        hi = jnp.clip(t + win // 2, 0, L)
        sl = slice(g * POOL_GROUP_DIM, (g + 1) * POOL_GROUP_DIM)
        s = cs[:, hi, sl] - cs[:, lo, sl]
        outs.append(s / (hi - lo).astype(jnp.float32)[None, :, None])
    pooled = jnp.concatenate(outs, axis=-1).astype(v.dtype) - v
    mixed = jnp.einsum('blgc,gcd->blgd', pooled.reshape(b, L, POOL_GROUPS, POOL_GROUP_DIM), w)
    return mixed.reshape(b, L, POOL_WIDTH) * scale


def _token_mixer(u_ctx, u_lat, rows, cols, w_in, conv_w, conv_b, rg_wa, rg_ba, rg_wx, rg_bx,
                 rg_lambda, da_lq1, da_lk1, da_lq2, da_lk2, da_norm, pool_w, pool_scale,
                 w_branch, w_out, lambda_init):
    p_ctx = u_ctx @ w_in
    p_lat = u_lat @ w_in
    a_sl = slice(OFF_A_X, OFF_A_X + RG_WIDTH)
    g_sl = slice(OFF_A_GATE, OFF_A_GATE + RG_WIDTH)
    qkv_sl = slice(OFF_QKV, OFF_QKV + 3 * DA_WIDTH)
    pool_sl = slice(OFF_POOL, OFF_POOL + POOL_WIDTH)
    ya_c, ya_l = _rglru_mixer(p_ctx[..., a_sl], p_ctx[..., g_sl], p_lat[..., a_sl], p_lat[..., g_sl],
                              conv_w, conv_b, rg_wa, rg_ba, rg_wx, rg_bx, rg_lambda)
    yb_c, yb_l = _diff_attention_mixer(p_ctx[..., qkv_sl], p_lat[..., qkv_sl], rows, cols,
                                       da_lq1, da_lk1, da_lq2, da_lk2, da_norm, lambda_init)
    yc_c = _pool_mixer(p_ctx[..., pool_sl], pool_w, pool_scale)
    yc_l = _pool_mixer(p_lat[..., pool_sl], pool_w, pool_scale)

    def merge(p, branches):
        merged = None
        for k, y in enumerate(branches):
            gate = jax.nn.sigmoid(p[..., OFF_GATES + k * D_MODEL:OFF_GATES + (k + 1) * D_MODEL])
            term = gate * (y @ w_branch[k])
            merged = term if merged is None else merged + term
        return merged @ w_out

    return merge(p_ctx, (ya_c, yb_c, yc_c)), merge(p_lat, (ya_l, yb_l, yc_l))


def _expert_choice_ffn(u, router, w1, w3, w2):
    b, n, d = u.shape
    cap = EC_FACTOR * n // N_EXPERTS
    logits = jnp.einsum('bnd,de->ben', u.astype(jnp.float32), router.astype(jnp.float32))
    aff = jax.nn.softmax(logits, axis=1)
    gate, idx = lax.top_k(aff, cap)
    xs = jax.vmap(lambda ub, ib: ub[ib])(u, idx)
    h = jax.nn.silu(jnp.einsum('becd,edf->becf', xs, w1)) * jnp.einsum('becd,edf->becf', xs, w3)
    y = jnp.einsum('becf,efd->becd', h, w2) * gate[..., None].astype(u.dtype)
    return jax.vmap(lambda yb, ib: jnp.zeros((n, d), yb.dtype).at[ib.reshape(-1)].add(yb.reshape(-1, d)))(y, idx)


def setup_inputs(seed: int = 0) -> dict:
    key = jax.random.key(seed)
    ks = jax.random.split(key, 40)
    L = DEPTH

    def nrm(k, shape, scale):
        return jax.random.normal(k, shape, jnp.float32) * scale

    a_pow = jax.random.uniform(ks[10], (L, 2, RG_WIDTH), jnp.float32, 0.9, 0.999)
    a_base = a_pow ** (1.0 / RG_C)
    rg_lambda = jnp.log(a_base) - jnp.log1p(-a_base)
    return {
        'x': nrm(ks[0], (BATCH, SEQ, D_MODEL), 1.0),
        'c': nrm(ks[1], (BATCH, D_MODEL), 1.0),
        'ctx': nrm(ks[2], (BATCH, CTX_LEN, D_MODEL), 1.0),
        'c_ctx': nrm(ks[3], (D_MODEL,), 1.0),
        'mod_a': nrm(ks[4], (L, D_MODEL, MOD_RANK), D_MODEL ** -0.5),
        'mod_b': nrm(ks[5], (L, MOD_RANK, N_MOD * D_MODEL), 0.1 * MOD_RANK ** -0.5),
        'mod_bias': nrm(ks[6], (L, N_MOD * D_MODEL), 0.01),
        'w_in': nrm(ks[7], (L, D_MODEL, IN_COLS), D_MODEL ** -0.5),
        'conv_w': nrm(ks[8], (L, CONV_W, RG_WIDTH), CONV_W ** -0.5),
        'conv_b': nrm(ks[9], (L, RG_WIDTH), 0.01),
        'rg_wa': nrm(ks[11], (L, 2, RG_BLOCKS, RG_BLOCK, RG_BLOCK), RG_BLOCK ** -0.5),
        'rg_ba': nrm(ks[12], (L, 2, RG_WIDTH), 0.01),
        'rg_wx': nrm(ks[13], (L, 2, RG_BLOCKS, RG_BLOCK, RG_BLOCK), RG_BLOCK ** -0.5),
        'rg_bx': nrm(ks[14], (L, 2, RG_WIDTH), 0.01),
        'rg_lambda': rg_lambda,
        'da_lq1': nrm(ks[15], (L, DA_HEAD_DIM), 0.1),
        'da_lk1': nrm(ks[16], (L, DA_HEAD_DIM), 0.1),
        'da_lq2': nrm(ks[17], (L, DA_HEAD_DIM), 0.1),
        'da_lk2': nrm(ks[18], (L, DA_HEAD_DIM), 0.1),
        'da_norm': 1.0 + nrm(ks[19], (L, 2 * DA_HEAD_DIM), 0.02),
        'pool_w': nrm(ks[20], (L, POOL_GROUPS, POOL_GROUP_DIM, POOL_GROUP_DIM), POOL_GROUP_DIM ** -0.5),
        'pool_scale': 1.0 + nrm(ks[21], (L, POOL_WIDTH), 0.02),
        'w_branch': nrm(ks[22], (L, N_BRANCH, BRANCH_WIDTH, D_MODEL), BRANCH_WIDTH ** -0.5),
        'w_out': nrm(ks[23], (L, D_MODEL, D_MODEL), DEEPNORM_BETA * D_MODEL ** -0.5),
        'ln1_g': 1.0 + nrm(ks[24], (L, D_MODEL), 0.02),
        'ln1_b': nrm(ks[25], (L, D_MODEL), 0.01),
        'router': nrm(ks[26], (L, D_MODEL, N_EXPERTS), D_MODEL ** -0.5),
        'ex_w1': nrm(ks[27], (L, N_EXPERTS, D_MODEL, EXPERT_FF), D_MODEL ** -0.5),
        'ex_w3': nrm(ks[28], (L, N_EXPERTS, D_MODEL, EXPERT_FF), D_MODEL ** -0.5),
        'ex_w2': nrm(ks[29], (L, N_EXPERTS, EXPERT_FF, D_MODEL), DEEPNORM_BETA * EXPERT_FF ** -0.5),
        'ln2_g': 1.0 + nrm(ks[30], (L, D_MODEL), 0.02),
        'ln2_b': nrm(ks[31], (L, D_MODEL), 0.01),
    }


def reference(x, c, ctx, c_ctx, mod_a, mod_b, mod_bias, w_in, conv_w, conv_b, rg_wa, rg_ba,
              rg_wx, rg_bx, rg_lambda, da_lq1, da_lk1, da_lq2, da_lk2, da_norm, pool_w,
              pool_scale, w_branch, w_out, ln1_g, ln1_b, router, ex_w1, ex_w3, ex_w2,
              ln2_g, ln2_b):
    n = x.shape[1]
    rows_n = n // GRID_W
    rows = jnp.repeat(jnp.arange(rows_n, dtype=jnp.int32), GRID_W)
    cols = jnp.tile(jnp.arange(GRID_W, dtype=jnp.int32), rows_n)
    s_c = jax.nn.silu(c)
    s_cc = jax.nn.silu(c_ctx)
    for l in range(DEPTH):
        lambda_init = 0.8 - 0.6 * math.exp(-0.3 * l)
        mod_l = ((s_c @ mod_a[l]) @ mod_b[l] + mod_bias[l]).reshape(-1, N_MOD, D_MODEL)[:, :, None, :]
        mod_c = ((s_cc @ mod_a[l]) @ mod_b[l] + mod_bias[l]).reshape(N_MOD, D_MODEL)[:, None, None, :]
        u_lat = _modulate(x, mod_l[:, 0], mod_l[:, 1])
        u_ctx = _modulate(ctx, mod_c[0], mod_c[1])
        mix_ctx, mix_lat = _token_mixer(u_ctx, u_lat, rows, cols, w_in[l], conv_w[l], conv_b[l],
                                        rg_wa[l], rg_ba[l], rg_wx[l], rg_bx[l], rg_lambda[l],
                                        da_lq1[l], da_lk1[l], da_lq2[l], da_lk2[l], da_norm[l],
                                        pool_w[l], pool_scale[l], w_branch[l], w_out[l], lambda_init)
        x = _layer_norm(DEEPNORM_ALPHA * x + (1.0 + mod_l[:, 2]) * mix_lat, ln1_g[l], ln1_b[l])
        u_lat = _modulate(x, mod_l[:, 3], mod_l[:, 4])
        ffn_lat = _expert_choice_ffn(u_lat, router[l], ex_w1[l], ex_w3[l], ex_w2[l])
        x = _layer_norm(DEEPNORM_ALPHA * x + (1.0 + mod_l[:, 5]) * ffn_lat, ln2_g[l], ln2_b[l])
        if l < DEPTH - 1:
            ctx = _layer_norm(DEEPNORM_ALPHA * ctx + (1.0 + mod_c[2]) * mix_ctx, ln1_g[l], ln1_b[l])
            u_ctx = _modulate(ctx, mod_c[3], mod_c[4])
            ffn_ctx = _expert_choice_ffn(u_ctx, router[l], ex_w1[l], ex_w3[l], ex_w2[l])
            ctx = _layer_norm(DEEPNORM_ALPHA * ctx + (1.0 + mod_c[5]) * ffn_ctx, ln2_g[l], ln2_b[l])
    return x
```

```python
import math
from contextlib import ExitStack

import numpy as np
import ml_dtypes
import concourse.bass as bass
import concourse.mybir as mybir
from concourse.bass_utils import run_bass_kernel_spmd

F32 = mybir.dt.float32
BF16 = mybir.dt.bfloat16
AF = mybir.ActivationFunctionType
ALU = mybir.AluOpType

D = 4096
NB = 8
SEQ = 2048
CTX = 256
T = CTX + SEQ
L_FULL = 4
BW = 1536
INC = 21504
NMOD = 6
NE = 16
FF = 384
KC = D // 128
ALPHA = (2 * L_FULL) ** 0.25
EPS = 1e-5
CHUNKS = [(0, 256), (256, 512), (768, 512), (1280, 512), (1792, 512)]
NV = 272
V_LN1G, V_LN1B, V_LN2G, V_LN2B, V_CW, V_CB, V_BA, V_BX, V_LAM, V_PS = 0, 32, 64, 96, 128, 176, 188, 212, 236, 260


class Sched:
    CE = ["pe", "act", "dve", "pool"]
    ALL = ["pe", "act", "dve", "pool", "sp"]

    def __init__(self, nc, es, ndma=40):
        self.nc = nc
        self.sem = {e: es.enter_context(nc.semaphore("s_" + e)) for e in self.CE}
        self.rel = [es.enter_context(nc.semaphore("s_rel%d" % i)) for i in range(3)]
        self.dsem = [es.enter_context(nc.semaphore("d%d" % i)) for i in range(ndma)]
        self.nbar = 0
        self.nstage = 0
        self._reset()

    def _reset(self):
        self.ops = {e: [] for e in self.ALL}
        self.cnt = {e: 0 for e in self.CE}
        self.dcnt = {}
        self.dkey = {}
        self.known = {e: {} for e in self.ALL}
        self.last_w = {}
        self.readers = {}

    def _deps(self, eng, reads, writes):
        deps = {}

        def add(tok):
            if tok is None:
                return
            sid, val = tok
            if deps.get(sid, 0) < val:
                deps[sid] = val

        for k in reads:
            add(self.last_w.get(k))
        for k in writes:
            add(self.last_w.get(k))
            for sid, val in self.readers.get(k, {}).items():
                add((sid, val))
        waits = []
        for sid, val in deps.items():
            if eng == "pe" and sid == ("e", "pe"):
                continue
            if self.known[eng].get(sid, 0) >= val:
                continue
            self.known[eng][sid] = val
            waits.append((sid, val))
        return waits

    def _mark(self, tok, reads, writes):
        for k in writes:
            self.last_w[k] = tok
            self.readers[k] = {}
        for k in reads:
            if k in writes:
                continue
            r = self.readers.setdefault(k, {})
            if r.get(tok[0], 0) < tok[1]:
                r[tok[0]] = tok[1]

    def op(self, eng, fn, reads=(), writes=(), inc=True):
        waits = self._deps(eng, reads, writes)
        if inc:
            self.cnt[eng] += 1
            tok = (("e", eng), self.cnt[eng])
        else:
            tok = (("e", eng), self.cnt[eng] + 1)
        self.ops[eng].append((waits, fn, ("e", eng) if inc else None))
        self._mark(tok, reads, writes)

    def dma(self, q, fn, key, reads=(), writes=()):
        waits = self._deps(q, reads, writes)
        if key not in self.dkey:
            self.dkey[key] = len(self.dkey)
            assert len(self.dkey) <= len(self.dsem), "out of DMA semaphores"
        i = self.dkey[key]
        self.dcnt[i] = self.dcnt.get(i, 0) + 16
        tok = (("d", i), self.dcnt[i])
        self.ops[q].append((waits, fn, ("d", i)))
        self._mark(tok, reads, writes)

    def _semof(self, sid):
        return self.sem[sid[1]] if sid[0] == "e" else self.dsem[sid[1]]

    def flush(self):
        nc = self.nc
        kb = self.nbar % 3
        self.nbar += 1
        used = [("e", e) for e in self.CE if self.cnt[e] > 0] + [("d", i) for i in self.dcnt]
        totals = {("e", e): self.cnt[e] for e in self.CE}
        totals.update({("d", i): v for i, v in self.dcnt.items()})
        engobj = {"pe": nc.tensor, "act": nc.scalar, "dve": nc.vector, "pool": nc.gpsimd, "sp": nc.sync}

        def body(ename):
            def f(eng):
                for waits, fn, incspec in self.ops[ename]:
                    for sid, val in waits:
                        eng.wait_ge(self._semof(sid), val)
                    ins = fn(eng)
                    if incspec is not None:
                        ins.then_inc(self._semof(incspec), 1 if incspec[0] == "e" else 16)
                if ename == "pool":
                    for sid in used:
                        eng.wait_ge(self._semof(sid), totals[sid])
                    import os
                    if "noclear" not in os.environ.get("KDBG", ""):
                        for sid in used:
                            eng.sem_clear(self._semof(sid))
                    eng.sem_clear(self.rel[(kb + 1) % 3])
                    eng.sem_inc(self.rel[kb], 1)
                else:
                    eng.wait_ge(self.rel[kb], 1)
            return f

        with nc.Block() as block:
            block.tensor(body("pe"))
            block.scalar(body("act"))
            block.vector(body("dve"))
            block.gpsimd(body("pool"))
            block.sync(body("sp"))
        self.nstage += 1
        self._reset()


class K:
    def __init__(self, depth, dump=(), stop_after=None):
        self.depth = depth
        self.dump = dump
        self.stop_after = stop_after
        self.nc = bass.Bass("TRN2", target_bir_lowering=False)
        self.ges = ExitStack()
        self.S = None

    def dram_in(self, name, shape, dt=F32):
        return self.nc.dram_tensor(name, list(shape), dt, kind="ExternalInput")

    def dram(self, name, shape, dt):
        return self.nc.dram_tensor(name, list(shape), dt)

    def sb(self, es, name, shape, dt):
        self._uid = getattr(self, "_uid", 0) + 1
        return es.enter_context(self.nc.sbuf_tensor("%s_%d" % (name, self._uid), list(shape), dt))

    def ps(self, es, name, shape=(128, 512), dt=F32):
        self._uid = getattr(self, "_uid", 0) + 1
        return es.enter_context(self.nc.psum_tensor("%s_%d" % (name, self._uid), list(shape), dt))

    def load(self, dst_ap, src_ap, key, q="sp", **kw):
        self.S.dma(q, lambda e: e.dma_start(out=dst_ap, in_=src_ap, **kw), key, writes=[key])

    def loadw(self, dst, src, n_mid, key, parts=4):
        step = (n_mid + parts - 1) // parts
        for m0 in range(0, n_mid, step):
            m1 = min(n_mid, m0 + step)
            self.load(dst[:, m0:m1, :], src[:, m0:m1, :], key)

    def store(self, dst_ap, src_ap, key, q="act", **kw):
        self.S.dma(q, lambda e: e.dma_start(out=dst_ap, in_=src_ap, **kw), key, reads=[key])

    def mm(self, out, lhsT, rhs, start, stop, reads, pkey, inc=None):
        self.S.op("pe", lambda e: e.matmul(out, lhsT=lhsT, rhs=rhs, start=start, stop=stop),
                  reads=reads, writes=[pkey], inc=(stop if inc is None else inc))

    def act(self, out, in_, func, reads, writes, **kw):
        self.S.op("act", lambda e: e.activation(out=out, in_=in_, func=func, **kw), reads=reads, writes=writes)

    def ts(self, out, in0, s1, s2, op0, op1, reads, writes, eng="dve"):
        if op1 is None:
            self.S.op(eng, lambda e: e.tensor_scalar(out=out, in0=in0, scalar1=s1, scalar2=None, op0=op0),
                      reads=reads, writes=writes)
        else:
            self.S.op(eng, lambda e: e.tensor_scalar(out=out, in0=in0, scalar1=s1, scalar2=s2, op0=op0, op1=op1),
                      reads=reads, writes=writes)

    def tt(self, out, in0, in1, op, reads, writes, eng="dve"):
        self.S.op(eng, lambda e: e.tensor_tensor(out=out, in0=in0, in1=in1, op=op), reads=reads, writes=writes)

    def stt(self, out, in0, scalar, in1, op0, op1, reads, writes):
        self.S.op("dve", lambda e: e.scalar_tensor_tensor(out=out, in0=in0, scalar=scalar, in1=in1, op0=op0, op1=op1),
                  reads=reads, writes=writes)

    def cp(self, out, in_, reads, writes, eng="dve"):
        if eng == "act":
            self.S.op("act", lambda e: e.copy(out=out, in_=in_), reads=reads, writes=writes)
        else:
            self.S.op(eng, lambda e: e.tensor_copy(out=out, in_=in_), reads=reads, writes=writes)

    def build(self):
        nc = self.nc
        Lr = self.depth
        ges = self.ges
        self.i_x = self.dram_in("xT", [KC, 128, T])
        self.i_cs = self.dram_in("cs", [128, KC, 2])
        self.i_moda = self.dram_in("mod_a", [Lr, D, 256])
        self.i_modb = self.dram_in("mod_b", [Lr, 256, NMOD * D])
        self.i_modbias = self.dram_in("mod_bias_t", [Lr, 128, NMOD * KC])
        self.i_vecs = self.dram_in("vecs", [Lr, 128, NV])
        self.i_dav = self.dram_in("dav", [Lr, 128, 768])
        self.i_win = self.dram_in("w_in", [Lr, INC // 128, 128, KC, 128])
        self.i_rgw = self.dram_in("rg_w", [Lr, 4, 12, 128, 128])
        self.i_poolw = self.dram_in("pool_w", [Lr, 4, 384, 384])
        self.i_wbr = self.dram_in("w_branch", [Lr, KC, 128, 36, 128])
        self.i_wout = self.dram_in("w_out", [Lr, KC, 128, KC, 128])
        self.i_router = self.dram_in("router", [Lr, D, 128])
        self.i_w1 = self.dram_in("ex_w1", [Lr, NE, 3, 128, KC, 128])
        self.i_w3 = self.dram_in("ex_w3", [Lr, NE, 3, 128, KC, 128])
        self.i_w2 = self.dram_in("ex_w2", [Lr, KC, 128, 48, 128])
        self.i_cf = self.dram_in("const_f", [128, 128 * 3 + 2 * T])
        self.i_cb = self.dram_in("const_b", [128, 128], BF16)
        self.i_sel = self.dram_in("const_sel", [128, NE * 128])
        self.i_rc = self.dram_in("const_rc", [4, 128, T])
        self.o_out = nc.dram_tensor("outT", [KC, 128, SEQ], F32, kind="ExternalOutput")
        self.b_win = [self.dram("b_win_p%d" % p_, [Lr, 56, 128, KC, 128], BF16).ap() for p_ in range(3)]
        self.b_rgw = self.dram("b_rgw", [Lr] + [4, 12, 128, 128], BF16).ap()
        self.b_poolw = self.dram("b_poolw", [Lr] + [4, 384, 384], BF16).ap()
        self.b_wbr = self.dram("b_wbr", [Lr, KC, 128, 36, 128], BF16).ap()
        self.b_wout = self.dram("b_wout", [Lr, KC, 128, KC, 128], BF16).ap()
        self.b_w1 = self.dram("b_w1", [Lr, NE, 3, 128, KC, 128], BF16).ap()
        self.b_w3 = self.dram("b_w3", [Lr, NE, 3, 128, KC, 128], BF16).ap()
        self.b_w2 = self.dram("b_w2", [Lr, KC, 128, 48, 128], BF16).ap()
        self.c_win = [self.dram("c_win_p%d" % p_, [56, 128, KC, 128], BF16).ap() for p_ in range(3)]
        self.c_rgw = self.dram("c_rgw", [4, 12, 128, 128], BF16).ap()
        self.c_poolw = self.dram("c_poolw", [4, 384, 384], BF16).ap()
        self.c_wbr = self.dram("c_wbr", [KC, 128, 36, 128], BF16).ap()
        self.c_wout = self.dram("c_wout", [KC, 128, KC, 128], BF16).ap()
        self.c_w1 = self.dram("c_w1", [NE, 3, 128, KC, 128], BF16).ap()
        self.c_w3 = self.dram("c_w3", [NE, 3, 128, KC, 128], BF16).ap()
        self.c_w2 = self.dram("c_w2", [KC, 128, 48, 128], BF16).ap()
        self.d_xs = self.dram("d_xs", [KC, 128, T], F32)
        self.d_u = self.dram("d_u", [KC, 128, T], BF16)
        self.d_xa = self.dram("d_xa", [12, 128, T], F32)
        self.d_ga = self.dram("d_ga", [12, 128, T], F32)
        self.d_qk = self.dram("d_qk", [24, 128, T], BF16)
        self.d_v = self.dram("d_v", [12, 128, T], BF16)
        self.d_pl = self.dram("d_pl", [12, 128, T], F32)
        self.d_g = self.dram("d_g", [96, 128, T], F32)
        self.d_y = self.dram("d_y", [36, 128, T], BF16)
        self.d_m = self.dram("d_m", [KC, 128, T], BF16)
        self.d_u2 = self.dram("d_u2", [KC, 128, T], BF16)
        self.d_u2f = self.dram("d_u2f", [KC, 128, T], F32)
        self.d_G = self.dram("d_G", [NE, T], F32)
        self.dumps = {}

        self.S = Sched(nc, ges)
        self.modL = self.sb(ges, "modL", [128, Lr, NMOD, KC], F32)
        self.modC = self.sb(ges, "modC", [128, Lr, NMOD, KC], F32)
        self.vecs = self.sb(ges, "vecs_sb", [128, Lr, NV], F32)
        self.dnorm = self.sb(ges, "dnorm", [128, Lr, 256], F32)
        self.lamv = self.sb(ges, "lamv", [128, Lr, 2], F32)
        self.c8 = self.sb(ges, "c8", [128, Lr, 24], F32)
        self.modLc = self.sb(ges, "modLc", [128, NMOD, KC], F32)
        self.modCc = self.sb(ges, "modCc", [128, NMOD, KC], F32)
        self.vecsc = self.sb(ges, "vecsc", [128, NV], F32)
        self.dnormc = self.sb(ges, "dnormc", [128, 256], F32)
        self.lamvc = self.sb(ges, "lamvc", [128, 2], F32)
        self.c8c = self.sb(ges, "c8c", [128, 24], F32)
        self.d_modL = self.dram("d_modL", [Lr, 128, NMOD * KC], F32)
        self.d_modC = self.dram("d_modC", [Lr, 128, NMOD * KC], F32)
        self.d_dnorm = self.dram("d_dnorm", [Lr, 128, 256], F32)
        self.d_lamv = self.dram("d_lamv", [Lr, 128, 2], F32)
        self.d_c8 = self.dram("d_c8", [Lr, 128, 24], F32)
        self.cf = self.sb(ges, "cf_sb", [128, 384], F32)
        self.cb = self.sb(ges, "cb_sb", [128, 128], BF16)
        self.sel = self.sb(ges, "sel_sb", [128, NE * 128], F32)

        import os
        dbg = os.environ.get("KDBG", "")
        if "nosetup" not in dbg:
            self.stage_setup()
        if "noprecast" not in dbg:
            self.stage_precast()
        stages = ["modulate", "win", "rglru", "pool", "attn", "merge", "ln1", "router", "ln2"]
        done = "nolayers" in dbg
        loop_cm = nc.Fori(0, Lr)
        lreg = loop_cm.__enter__()
        for l in [lreg]:
            last = False
            for st in stages:
                if done:
                    break
                if "only=" in dbg and ("only=" + st) not in dbg:
                    if self.stop_after is not None and self.stop_after[1] == st:
                        done = True
                    continue
                if st == "modulate":
                    self.stage_modulate(l)
                elif st == "win":
                    self.stage_win(l)
                elif st == "rglru":
                    self.stage_rglru(l)
                elif st == "pool":
                    self.stage_pool(l)
                elif st == "attn":
                    self.stage_attn(l)
                elif st == "merge":
                    self.stage_merge(l, last)
                elif st == "ln1":
                    self.stage_proj_ln(l, last, which=1)
                elif st == "router":
                    self.stage_router(l, last)
                elif st == "ln2":
                    self.stage_proj_ln(l, last, which=2)
                if self.stop_after is not None and self.stop_after[1] == st:
                    done = True
        loop_cm.__exit__(None, None, None)
        self.stage_final()
        self.stage_output()
        ges.close()
        return nc

    def lambda_init(self, l):
        return 0.8 - 0.6 * math.exp(-0.3 * l)

    def stage_setup(self):
        S = self.S
        Lr = self.depth
        with ExitStack() as es:
            cs = self.sb(es, "cs_sb", [128, KC, 2], F32)
            ss = self.sb(es, "ss_sb", [128, KC, 2], F32)
            ma = self.sb(es, "ma_sb", [128, KC, 256], F32)
            mb = self.sb(es, "mb_sb", [128, 2, D], F32)
            mbias = self.sb(es, "mbias_sb", [128, NMOD * KC], F32)
            hs = self.sb(es, "hs_sb", [128, 4], F32)
            dav = self.sb(es, "dav_sb", [128, 768], F32)
            tmp = self.sb(es, "tmp_sb", [128, 128], F32)
            red = self.sb(es, "red_sb", [128, 4], F32)
            xe = self.sb(es, "xe_sb", [128, 24], F32)
            pl = self.sb(es, "pl_sb", [128, 24], F32)
            p1 = self.ps(es, "p1", [128, 512])
            p2 = self.ps(es, "p2", [128, 512])
            self.load(self.cf[:, :], self.i_cf[:, 0:384], "cf_sb")
            self.load(self.cb[:, :], self.i_cb[:, :], "cb_sb")
            self.load(self.sel[:, :], self.i_sel[:, :], "sel_sb")
            self.load(cs[:, :, :], self.i_cs[:, :, :], "cs_sb")
            for l in range(Lr):
                self.load(self.vecs[:, l, :], self.i_vecs[l, :, :], "vecs_sb")
            self.act(ss[:, :, :], cs[:, :, :], AF.Silu, ["cs_sb"], ["ss_sb"])
            for l in range(Lr):
                self.load(ma[:, :, :], self.i_moda[l].rearrange("(kc p) r -> p kc r", p=128), "ma_sb")
                self.load(mbias[:, :], self.i_modbias[l, :, :], "mbias_sb")
                for rt in range(2):
                    for kc in range(KC):
                        self.mm(p1[:, 2 * rt:2 * rt + 2], ma[:, kc, rt * 128:(rt + 1) * 128], ss[:, kc, :],
                                kc == 0, kc == KC - 1, ["ma_sb", "ss_sb"], "p1")
                self.cp(hs[:, :], p1[:, 0:4], ["p1"], ["hs_sb"])
                for j in range(NMOD):
                    self.load(mb[:, :, :], self.i_modb[l, :, j * D:(j + 1) * D].rearrange("(rc p) f -> p rc f", p=128), "mb_sb")
                    for c in range(KC):
                        for rc in range(2):
                            self.mm(p2[:, 2 * c:2 * c + 2], mb[:, rc, c * 128:(c + 1) * 128], hs[:, 2 * rc:2 * rc + 2],
                                    rc == 0, rc == 1, ["mb_sb", "hs_sb"], "p2")
                    add1 = 1.0 if j in (1, 2, 4, 5) else 0.0
                    for col, dst, dk in ((0, self.modL, "modL"), (1, self.modC, "modC")):
                        self.tt(dst[:, l, j, :], p2[:, col:2 * KC:2], mbias[:, j * KC:(j + 1) * KC], ALU.add,
                                ["p2", "mbias_sb"], [dk])
                        if add1:
                            self.ts(dst[:, l, j, :], dst[:, l, j, :], 1.0, None, ALU.add, None, [dk], [dk])
                self.load(dav[:, :], self.i_dav[l, :, :], "dav_sb")
                for i in range(2):
                    self.tt(tmp[:, :], dav[:, 256 * i:256 * i + 128], dav[:, 256 * i + 128:256 * i + 256], ALU.mult,
                            ["dav_sb"], ["tmp_sb"])
                    S.op("dve", lambda e, i=i: e.reduce_sum(out=red[:, i:i + 1], in_=tmp[:, :], axis=mybir.AxisListType.X),
                         reads=["tmp_sb"], writes=["red_sb"])
                self.act(red[:, 2:4], red[:, 0:2], AF.Exp, ["red_sb"], ["red_sb"])
                self.tt(self.lamv[:, l, 0:1], red[:, 2:3], red[:, 3:4], ALU.subtract, ["red_sb"], ["lamv"])
                self.ts(self.lamv[:, l, 0:1], self.lamv[:, l, 0:1], float(self.lambda_init(l)), None, ALU.add, None, ["lamv"], ["lamv"])
                self.ts(self.lamv[:, l, 1:2], self.lamv[:, l, 0:1], -1.0, None, ALU.mult, None, ["lamv"], ["lamv"])
                self.ts(self.dnorm[:, l, :], dav[:, 512:768], float(1.0 - self.lambda_init(l)), None, ALU.mult, None,
                        ["dav_sb"], ["dnorm"])
                self.act(xe[:, :], self.vecs[:, l, V_LAM:V_LAM + 24], AF.Exp, ["vecs_sb"], ["xe_sb"], scale=-1.0)
                self.ts(pl[:, :], xe[:, :], -0.25, 1.0 / 3.0, ALU.mult, ALU.add, ["xe_sb"], ["pl_sb"])
                self.tt(pl[:, :], pl[:, :], xe[:, :], ALU.mult, ["pl_sb", "xe_sb"], ["pl_sb"])
                self.ts(pl[:, :], pl[:, :], -1.0, 0.5, ALU.mult, ALU.add, ["pl_sb"], ["pl_sb"])
                self.tt(pl[:, :], pl[:, :], xe[:, :], ALU.mult, ["pl_sb", "xe_sb"], ["pl_sb"])
                self.ts(pl[:, :], pl[:, :], -1.0, 1.0, ALU.mult, ALU.add, ["pl_sb"], ["pl_sb"])
                self.tt(pl[:, :], pl[:, :], xe[:, :], ALU.mult, ["pl_sb", "xe_sb"], ["pl_sb"])
                self.ts(self.c8[:, l, :], pl[:, :], -8.0, None, ALU.mult, None, ["pl_sb"], ["c8"])
            for l in range(Lr):
                self.store(self.d_modL[l].rearrange("p (j c) -> p j c", j=NMOD), self.modL[:, l, :, :], "modL")
                self.store(self.d_modC[l].rearrange("p (j c) -> p j c", j=NMOD), self.modC[:, l, :, :], "modC")
                self.store(self.d_dnorm[l], self.dnorm[:, l, :], "dnorm")
                self.store(self.d_lamv[l], self.lamv[:, l, :], "lamv")
                self.store(self.d_c8[l], self.c8[:, l, :], "c8")
            self.S.dma("sp", lambda e: e.dma_start(out=self.d_xs[0:16], in_=self.i_x[0:16]), "xcopy0")
            self.S.dma("sp", lambda e: e.dma_start(out=self.d_xs[16:32], in_=self.i_x[16:32]), "xcopy1")
            S.flush()

    def stage_precast(self):
        S = self.S
        Lr = self.depth
        import os
        with ExitStack() as es:
            fi = [self.sb(es, "pc_f%d" % i, [128, 4096], F32) for i in range(2)]
            bo = [self.sb(es, "pc_b%d" % i, [128, 4096], BF16) for i in range(2)]
            it = 0

            def flat(ap, pat):
                return ap.rearrange(pat).rearrange("(r c) -> r c", c=4096)

            P4 = "a p k c -> (a p k c)"
            P5 = "e t p k c -> (e t p k c)"
            for l in range(Lr):
                jobs = [(self.b_win[part][l], self.i_win[l, part * 56:(part + 1) * 56], P4, 56 * 128 * KC * 128) for part in range(3)]
                jobs += [
                    (self.b_rgw[l], self.i_rgw[l], "a b i j -> (a b i j)", 48 * 128 * 128),
                    (self.b_poolw[l], self.i_poolw[l], "g c d -> (g c d)", 4 * 384 * 384),
                    (self.b_wbr[l], self.i_wbr[l], P4, 3 * BW * D),
                    (self.b_wout[l], self.i_wout[l], P4, D * D),
                    (self.b_w1[l], self.i_w1[l], P5, NE * D * FF),
                    (self.b_w3[l], self.i_w3[l], P5, NE * D * FF),
                    (self.b_w2[l], self.i_w2[l], P4, NE * FF * D),
                ]
                for dst, src, pat, n in jobs:
                    d2, s2 = flat(dst, pat), flat(src, pat)
                    rows = n // 4096
                    for r0 in range(0, rows, 128):
                        r1 = min(rows, r0 + 128)
                        nr = r1 - r0
                        b = it % 2
                        it += 1
                        self.load(fi[b][0:nr, :], s2[r0:r1, :], "pc_f%d" % b)
                        self.cp(bo[b][0:nr, :], fi[b][0:nr, :], ["pc_f%d" % b], ["pc_b%d" % b], eng=("dve" if b == 0 else "pool"))
                        self.store(d2[r0:r1, :], bo[b][0:nr, :], "pc_b%d" % b)
            S.flush()

    def xsrc(self, l):
        return self.d_xs

    def stage_modulate(self, l):
        S = self.S
        src = self.xsrc(l)
        with ExitStack() as es:
            xin = [self.sb(es, "mx%d" % i, [128, T], F32) for i in range(2)]
            uo = [self.sb(es, "mu%d" % i, [128, T], BF16) for i in range(2)]
            def flat(ap, pat):
                return ap.rearrange(pat).rearrange("(r c) -> r c", c=4096)
            for p_ in range(3):
                S.dma("sp", lambda e, p_=p_: e.dma_start(out=flat(self.c_win[p_], "a p k c -> (a p k c)"), in_=flat(self.b_win[p_][l], "a p k c -> (a p k c)")), "wc%d" % p_)
            for i_, (cd, bs, pat) in enumerate(((self.c_rgw, self.b_rgw, "a b i j -> (a b i j)"), (self.c_poolw, self.b_poolw, "g c d -> (g c d)"),
                                              (self.c_wbr, self.b_wbr, "a p k c -> (a p k c)"), (self.c_wout, self.b_wout, "a p k c -> (a p k c)"),
                                              (self.c_w1, self.b_w1, "e t p k c -> (e t p k c)"), (self.c_w3, self.b_w3, "e t p k c -> (e t p k c)"),
                                              (self.c_w2, self.b_w2, "a p k c -> (a p k c)"))):
                S.dma("act", lambda e, cd=cd, bs=bs, pat=pat: e.dma_start(out=flat(cd, pat), in_=flat(bs[l], pat)), "wd%d" % i_)
            self.load(self.modLc[:, :, :], self.d_modL[l].rearrange("p (j c) -> p j c", j=NMOD), "modL")
            self.load(self.modCc[:, :, :], self.d_modC[l].rearrange("p (j c) -> p j c", j=NMOD), "modC")
            self.load(self.vecsc[:, :], self.i_vecs[l], "vecs_sb")
            self.load(self.dnormc[:, :], self.d_dnorm[l], "dnorm")
            self.load(self.lamvc[:, :], self.d_lamv[l], "lamv")
            self.load(self.c8c[:, :], self.d_c8[l], "c8")
            for c in range(KC):
                b = c % 2
                self.load(xin[b][:, :], src[c, :, :], "mx%d" % b)
                self.ts(uo[b][:, 0:CTX], xin[b][:, 0:CTX], self.modCc[:, 1, c:c + 1], self.modCc[:, 0, c:c + 1],
                        ALU.mult, ALU.add, ["mx%d" % b, "modC"], ["mu%d" % b])
                self.ts(uo[b][:, CTX:T], xin[b][:, CTX:T], self.modLc[:, 1, c:c + 1], self.modLc[:, 0, c:c + 1],
                        ALU.mult, ALU.add, ["mx%d" % b, "modL"], ["mu%d" % b])
                self.store(self.d_u[c, :, :], uo[b][:, :], "mu%d" % b)
            S.flush()

    def stage_win(self, l):
        S = self.S
        TP = 1280
        with ExitStack() as es:
            ut = self.sb(es, "ut", [128, KC, TP], BF16)
            wb = [self.sb(es, "wb%d" % i, [128, KC, 128], BF16) for i in range(3)]
            of = [self.sb(es, "of%d" % i, [128, 512], F32) for i in range(3)]
            ob = [self.sb(es, "ob%d" % i, [128, 512], BF16) for i in range(3)]
            qf = [self.sb(es, "qf%d" % i, [128, 512], F32) for i in range(2)]
            t1 = [self.sb(es, "t1%d" % i, [128, 512], F32) for i in range(2)]
            rope = self.sb(es, "rope", [128, 2 * T], F32)
            pp = [self.ps(es, "pp%d" % i) for i in range(4)]
            pq = [self.ps(es, "pq%d" % i) for i in range(2)]
            self.load(rope[:, :], self.i_cf[:, 384:384 + 2 * T], "rope")
            nblk = INC // 128
            it = 0
            io = 0
            for (pass_chunks, base, ntok) in ((CHUNKS[0:3], 0, 1280), (CHUNKS[3:5], 1280, 1024)):
                for kc in range(KC):
                    self.load(ut[:, kc, 0:ntok], self.d_u[kc, :, base:base + ntok], "ut")
                for j in range(nblk):
                    w = wb[j % 3]
                    wk = "wb%d" % (j % 3)
                    self.load(w[:, :, :], self.c_win[j // 56][j % 56], wk)
                    sec = j // 12
                    for ci, (t0, tn) in enumerate(pass_chunks):
                        p = pp[it % 4]
                        pk = "pp%d" % (it % 4)
                        it += 1
                        b = io % 3
                        io += 1
                        ofk, obk = "of%d" % b, "ob%d" % b
                        u0 = t0 - base
                        for kc in range(KC):
                            self.mm(p[:, 0:tn], w[:, kc, :], ut[:, kc, u0:u0 + tn], kc == 0, kc == KC - 1, [wk, "ut"], pk)
                        if sec == 0 or sec == 5:
                            self.cp(of[b][:, 0:tn], p[:, 0:tn], [pk], [ofk], eng="act")
                        elif sec == 1:
                            tb = t1[ci % 2]
                            tk = "t1%d" % (ci % 2)
                            self.act(tb[:, 0:tn], p[:, 0:tn], AF.Square, [pk], [tk])
                            self.ts(tb[:, 0:tn], tb[:, 0:tn], 0.044715, 1.0, ALU.mult, ALU.add, [tk], [tk])
                            self.tt(tb[:, 0:tn], tb[:, 0:tn], p[:, 0:tn], ALU.mult, [tk, pk], [tk])
                            self.act(tb[:, 0:tn], tb[:, 0:tn], AF.Sigmoid, [tk], [tk], scale=2.0 * math.sqrt(2.0 / math.pi))
                            self.tt(of[b][:, 0:tn], tb[:, 0:tn], p[:, 0:tn], ALU.mult, [tk, pk], [ofk])
                        elif sec in (2, 3):
                            qb = qf[ci % 2]
                            qk_ = "qf%d" % (ci % 2)
                            tb = t1[ci % 2]
                            tk = "t1%d" % (ci % 2)
                            p2 = pq[ci % 2]
                            p2k = "pq%d" % (ci % 2)
                            self.cp(qb[:, 0:tn], p[:, 0:tn], [pk], [qk_], eng="act")
                            self.mm(p2[:, 0:tn], self.cf[:, 128:256], qb[:, 0:tn], True, True, ["cf_sb", qk_], p2k)
                            self.tt(tb[:, 0:tn], qb[:, 0:tn], rope[:, t0:t0 + tn], ALU.mult, [qk_, "rope"], [tk])
                            self.tt(qb[:, 0:tn], p2[:, 0:tn], rope[:, T + t0:T + t0 + tn], ALU.mult, [p2k, "rope"], [qk_])
                            self.tt(ob[b][:, 0:tn], tb[:, 0:tn], qb[:, 0:tn], ALU.add, [tk, qk_], [obk])
                        elif sec == 4:
                            self.cp(ob[b][:, 0:tn], p[:, 0:tn], [pk], [obk], eng="act")
                        else:
                            self.act(of[b][:, 0:tn], p[:, 0:tn], AF.Sigmoid, [pk], [ofk])
                        if sec == 0:
                            self.store(self.d_xa[j, :, t0:t0 + tn], of[b][:, 0:tn], ofk)
                        elif sec == 1:
                            self.store(self.d_ga[j - 12, :, t0:t0 + tn], of[b][:, 0:tn], ofk)
                        elif sec in (2, 3):
                            self.store(self.d_qk[j - 24, :, t0:t0 + tn], ob[b][:, 0:tn], obk)
                        elif sec == 4:
                            self.store(self.d_v[j - 48, :, t0:t0 + tn], ob[b][:, 0:tn], obk)
                        elif sec == 5:
                            self.store(self.d_pl[j - 60, :, t0:t0 + tn], of[b][:, 0:tn], ofk)
                        else:
                            self.store(self.d_g[j - 72, :, t0:t0 + tn], of[b][:, 0:tn], ofk)
            S.flush()

    def stage_rglru(self, l):
        S = self.S
        vec = self.vecsc
        SEG = [(0, CTX), (CTX, SEQ)]
        with ExitStack() as es:
            xa = self.sb(es, "r_xa", [128, T], F32)
            ga = self.sb(es, "r_ga", [128, T], F32)
            xc = self.sb(es, "r_xc", [128, T], F32)
            xcb = self.sb(es, "r_xcb", [128, T], BF16)
            wg = self.sb(es, "r_wg", [128, 4, 128], BF16)
            av = self.sb(es, "r_a", [128, T], F32)
            bv = self.sb(es, "r_b", [128, T], F32)
            tv = self.sb(es, "r_t", [128, T], F32)
            hv = [self.sb(es, "r_h%d" % i, [128, T], F32) for i in range(2)]
            yo = self.sb(es, "r_y", [128, T], BF16)
            pr = [self.ps(es, "r_p%d" % i) for i in range(4)]
            it = 0
            for cb in range(12):
                self.load(xa[:, :], self.d_xa[cb, :, :], "r_xa")
                self.load(ga[:, :], self.d_ga[cb, :, :], "r_ga")
                self.load(wg[:, :, :], self.c_rgw[:, cb, :, :].rearrange("a i j -> i a j"), "r_wg")
                cw = lambda k: vec[:, V_CW + k * 12 + cb:V_CW + k * 12 + cb + 1]
                for (s0, sn) in SEG:
                    self.ts(xc[:, s0:s0 + sn], xa[:, s0:s0 + sn], cw(1), vec[:, V_CB + cb:V_CB + cb + 1], ALU.mult, ALU.add,
                            ["r_xa", "vecs_sb"], ["r_xc"])
                    self.stt(xc[:, s0 + 1:s0 + sn], xa[:, s0:s0 + sn - 1], cw(0), xc[:, s0 + 1:s0 + sn], ALU.mult, ALU.add,
                             ["r_xa", "r_xc", "vecs_sb"], ["r_xc"])
                    self.stt(xc[:, s0:s0 + sn - 1], xa[:, s0 + 1:s0 + sn], cw(2), xc[:, s0:s0 + sn - 1], ALU.mult, ALU.add,
                             ["r_xa", "r_xc", "vecs_sb"], ["r_xc"])
                    self.stt(xc[:, s0:s0 + sn - 2], xa[:, s0 + 2:s0 + sn], cw(3), xc[:, s0:s0 + sn - 2], ALU.mult, ALU.add,
                             ["r_xa", "r_xc", "vecs_sb"], ["r_xc"])
                self.cp(xcb[:, :], xc[:, :], ["r_xc"], ["r_xcb"], eng="pool")
                for d in range(2):
                    for gi, dst, dk, bias0 in ((0, av, "r_a", V_BA), (1, bv, "r_b", V_BX)):
                        for (t0, tn) in CHUNKS:
                            p = pr[it % 4]
                            pk = "r_p%d" % (it % 4)
                            it += 1
                            self.mm(p[:, 0:tn], wg[:, gi * 2 + d, :], xcb[:, t0:t0 + tn], True, True, ["r_wg", "r_xcb"], pk)
                            self.act(dst[:, t0:t0 + tn], p[:, 0:tn], AF.Sigmoid, [pk, "vecs_sb"], [dk],
                                     bias=vec[:, bias0 + d * 12 + cb:bias0 + d * 12 + cb + 1])
                    self.act(av[:, :], av[:, :], AF.Exp, ["r_a", "c8"], ["r_a"], scale=self.c8c[:, d * 12 + cb:d * 12 + cb + 1])
                    self.tt(tv[:, :], av[:, :], av[:, :], ALU.mult, ["r_a"], ["r_t"])
                    self.ts(tv[:, :], tv[:, :], -1.0, 1.0, ALU.mult, ALU.add, ["r_t"], ["r_t"])
                    self.act(tv[:, :], tv[:, :], AF.Sqrt, ["r_t"], ["r_t"])
                    self.tt(bv[:, :], bv[:, :], xc[:, :], ALU.mult, ["r_b", "r_xc"], ["r_b"])
                    self.tt(bv[:, :], bv[:, :], tv[:, :], ALU.mult, ["r_b", "r_t"], ["r_b"])
                    h = hv[d]
                    hk = "r_h%d" % d
                    if d == 0:
                        S.op("dve", lambda e, h=h: e.tensor_tensor_scan(out=h[:, 0:CTX], data0=av[:, 0:CTX], data1=bv[:, 0:CTX],
                                                                          initial=0.0, op0=ALU.mult, op1=ALU.add),
                             reads=["r_a", "r_b"], writes=[hk])
                        S.op("dve", lambda e, h=h: e.tensor_tensor_scan(out=h[:, CTX:T], data0=av[:, CTX:T], data1=bv[:, CTX:T],
                                                                          initial=h[:, CTX - 1:CTX], op0=ALU.mult, op1=ALU.add),
                             reads=["r_a", "r_b", hk], writes=[hk])
                    else:
                        S.op("dve", lambda e, h=h: e.tensor_tensor_scan(out=h[:, 0:CTX][:, ::-1], data0=av[:, 0:CTX][:, ::-1],
                                                                          data1=bv[:, 0:CTX][:, ::-1],
                                                                          initial=0.0, op0=ALU.mult, op1=ALU.add),
                             reads=["r_a", "r_b"], writes=[hk])
                        S.op("dve", lambda e, h=h: e.tensor_tensor_scan(out=h[:, CTX:T][:, ::-1], data0=av[:, CTX:T][:, ::-1],
                                                                          data1=bv[:, CTX:T][:, ::-1],
                                                                          initial=h[:, 0:1], op0=ALU.mult, op1=ALU.add),
                             reads=["r_a", "r_b", hk], writes=[hk])
                self.tt(tv[:, :], hv[0][:, :], hv[1][:, :], ALU.add, ["r_h0", "r_h1"], ["r_t"])
                self.tt(yo[:, :], tv[:, :], ga[:, :], ALU.mult, ["r_t", "r_ga"], ["r_y"])
                self.store(self.d_y[cb, :, :], yo[:, :], "r_y")
            S.flush()

    def stage_pool(self, l):
        S = self.S
        PADL = 16
        WT = T + 64
        with ExitStack() as es:
            vraw = self.sb(es, "c_v", [128, T], F32)
            bufs = [self.sb(es, "c_s%d" % i, [128, 2, SEQ + 32], F32) for i in range(2)]
            rc = self.sb(es, "c_rc", [128, T], F32)
            pb = self.sb(es, "c_pb", [128, 3, T], BF16)
            pw = self.sb(es, "c_pw", [128, 3, 384], BF16)
            yo = [self.sb(es, "c_y%d" % i, [128, T], BF16) for i in range(2)]
            pr = [self.ps(es, "c_p%d" % i) for i in range(4)]
            for i in range(2):
                S.op("pool", lambda e, i=i: e.memset(bufs[i][:, :, :], 0.0), writes=["c_s%d" % i])
            it = 0
            for g in range(4):
                win = (2, 4, 8, 16)[g]
                self.load(rc[:, :], self.i_rc[g, :, :], "c_rc")
                self.load(pw[:, :, :], self.c_poolw[g, :, :].rearrange("(cc p) d -> p cc d", p=128), "c_pw")
                for bi in range(3):
                    cb = g * 3 + bi
                    self.load(vraw[:, :], self.d_pl[cb, :, :], "c_v")
                    A, Bf = bufs[0], bufs[1]
                    for si, (s0, sn) in enumerate(((0, CTX), (CTX, SEQ))):
                        self.cp(A[:, si, PADL:PADL + sn], vraw[:, s0:s0 + sn], ["c_v"], ["c_s0"], eng="pool")
                    cur, nxt, ck, nk = A, Bf, "c_s0", "c_s1"
                    Lmax = SEQ + 32
                    w = 1
                    while w < win:
                        if w == 1:
                            lo, hi = 1, Lmax
                            self.tt(nxt[:, :, lo:hi], cur[:, :, lo - 1:hi - 1], cur[:, :, lo:hi], ALU.add, [ck], [nk])
                        else:
                            hsh = w // 2
                            lo, hi = w, Lmax - w
                            self.tt(nxt[:, :, lo:hi], cur[:, :, lo - hsh:hi - hsh], cur[:, :, lo + hsh:hi + hsh], ALU.add, [ck], [nk])
                        cur, nxt, ck, nk = nxt, cur, nk, ck
                        w *= 2
                    for si, (s0, sn) in enumerate(((0, CTX), (CTX, SEQ))):
                        self.tt(cur[:, si, PADL:PADL + sn], cur[:, si, PADL:PADL + sn], rc[:, s0:s0 + sn], ALU.mult, [ck, "c_rc"], [ck])
                        self.tt(pb[:, bi, s0:s0 + sn], cur[:, si, PADL:PADL + sn], vraw[:, s0:s0 + sn], ALU.subtract,
                                [ck, "c_v"], ["c_pb"])
                    for i in range(2):
                        S.op("pool", lambda e, i=i: e.memset(bufs[i][:, :, :], 0.0), writes=["c_s%d" % i])
                for dtile in range(3):
                    b = dtile % 2
                    for (t0, tn) in CHUNKS:
                        p = pr[it % 4]
                        pk = "c_p%d" % (it % 4)
                        it += 1
                        for cc in range(3):
                            self.mm(p[:, 0:tn], pw[:, cc, dtile * 128:(dtile + 1) * 128], pb[:, cc, t0:t0 + tn],
                                    cc == 0, cc == 2, ["c_pw", "c_pb"], pk)
                        cbo = g * 3 + dtile
                        self.ts(yo[b][:, t0:t0 + tn], p[:, 0:tn], self.vecsc[:, V_PS + cbo:V_PS + cbo + 1], None, ALU.mult, None,
                                [pk, "vecs_sb"], ["c_y%d" % b])
                    self.store(self.d_y[24 + g * 3 + dtile, :, :], yo[b][:, :], "c_y%d" % b)
            S.flush()

    def stage_attn(self, l):
        S = self.S
        sc = 1.0 / math.sqrt(128.0)
        NT = T // 128
        with ExitStack() as es:
            qk = self.sb(es, "a_qk", [128, 4, T], BF16)
            vT = self.sb(es, "a_vT", [128, 2, T], BF16)
            vtm = self.sb(es, "a_vtm", [128, NT, 264], BF16)
            et = [self.sb(es, "a_e%d" % i, [128, 512], BF16) for i in range(3)]
            o0 = [self.sb(es, "a_o%d" % i, [128, 264], F32) for i in range(4)]
            of = self.sb(es, "a_of", [128, 256], F32)
            sq = self.sb(es, "a_sq", [128, 256], F32)
            sm = self.sb(es, "a_sm", [128, 8], F32)
            ob = self.sb(es, "a_ob", [128, 256], BF16)
            yb = self.sb(es, "a_yb", [128, 2, T], BF16)
            pst = [self.ps(es, "a_ps%d" % i) for i in range(2)]
            pov = [self.ps(es, "a_po%d" % i) for i in range(4)]
            ptr = [self.ps(es, "a_pt%d" % i, [128, 512], BF16) for i in range(2)]
            S.op("pool", lambda e: e.memset(vtm[:, :, :], 1.0), writes=["a_vtm"])
            ie = 0
            itr = 0
            for h in range(6):
                for c in range(2):
                    self.load(qk[:, c, :], self.d_qk[h * 2 + c, :, :], "a_qk")
                    self.load(qk[:, 2 + c, :], self.d_qk[12 + h * 2 + c, :, :], "a_qk")
                    self.load(vT[:, c, :], self.d_v[h * 2 + c, :, :], "a_vT")
                for tt_ in range(NT):
                    pt = ptr[itr % 2]
                    ptk = "a_pt%d" % (itr % 2)
                    itr += 1
                    for eb in range(2):
                        S.op("pe", lambda e, pt=pt, eb=eb, tt_=tt_: e.transpose(out=pt[:, eb * 128:(eb + 1) * 128],
                                                                                in_=vT[:, eb, tt_ * 128:(tt_ + 1) * 128],
                                                                                identity=self.cb[:, :]),
                             reads=["a_vT", "cb_sb"], writes=[ptk])
                    self.cp(vtm[:, tt_, 0:256], pt[:, 0:256], [ptk], ["a_vtm"], eng="act")
                for ci, (t0, tn) in enumerate(CHUNKS):
                    nk = 2 if ci == 0 else NT
                    nq = tn // 128
                    for c in range(2):
                        for kt in range(nk):
                            p = pst[ie % 2]
                            pk = "a_ps%d" % (ie % 2)
                            e_ = et[ie % 3]
                            ek = "a_e%d" % (ie % 3)
                            ie += 1
                            self.mm(p[:, 0:tn], qk[:, 2 + c, kt * 128:(kt + 1) * 128], qk[:, c, t0:t0 + tn], True, True, ["a_qk"], pk)
                            self.act(e_[:, 0:tn], p[:, 0:tn], AF.Exp, [pk], [ek], scale=sc)
                            for qi in range(nq):
                                self.mm(pov[qi][:, 0:257], e_[:, qi * 128:(qi + 1) * 128], vtm[:, kt, 0:257],
                                        kt == 0, kt == nk - 1, [ek, "a_vtm"], "a_po%d" % qi)
                        if c == 0:
                            for qi in range(nq):
                                self.cp(o0[qi][:, 0:257], pov[qi][:, 0:257], ["a_po%d" % qi], ["a_o%d" % qi], eng="act")
                        else:
                            for qi in range(nq):
                                ok = "a_o%d" % qi
                                pok = "a_po%d" % qi
                                S.op("dve", lambda e, qi=qi: e.reciprocal(out=sm[:, 0:1], in_=o0[qi][:, 256:257]),
                                     reads=[ok], writes=["a_sm"])
                                S.op("dve", lambda e, qi=qi: e.reciprocal(out=sm[:, 1:2], in_=pov[qi][:, 256:257]),
                                     reads=[pok], writes=["a_sm"])
                                self.tt(sm[:, 1:2], sm[:, 1:2], self.lamvc[:, 1:2], ALU.mult, ["a_sm", "lamv"], ["a_sm"])
                                self.ts(of[:, :], o0[qi][:, 0:256], sm[:, 0:1], None, ALU.mult, None, [ok, "a_sm"], ["a_of"])
                                self.stt(of[:, :], pov[qi][:, 0:256], sm[:, 1:2], of[:, :], ALU.mult, ALU.add, [pok, "a_sm", "a_of"], ["a_of"])
                                self.act(sq[:, :], of[:, :], AF.Square, ["a_of"], ["a_sq"])
                                S.op("dve", lambda e: e.reduce_sum(out=sm[:, 2:3], in_=sq[:, :], axis=mybir.AxisListType.X),
                                     reads=["a_sq"], writes=["a_sm"])
                                self.ts(sm[:, 2:3], sm[:, 2:3], 1.0 / 256.0, EPS, ALU.mult, ALU.add, ["a_sm"], ["a_sm"])
                                self.act(sm[:, 3:4], sm[:, 2:3], AF.Sqrt, ["a_sm"], ["a_sm"])
                                S.op("dve", lambda e: e.reciprocal(out=sm[:, 4:5], in_=sm[:, 3:4]), reads=["a_sm"], writes=["a_sm"])
                                self.stt(ob[:, :], of[:, :], sm[:, 4:5], self.dnormc[:, :], ALU.mult, ALU.mult,
                                         ["a_of", "a_sm", "dnorm"], ["a_ob"])
                                pt = ptr[itr % 2]
                                ptk = "a_pt%d" % (itr % 2)
                                itr += 1
                                for eb in range(2):
                                    S.op("pe", lambda e, pt=pt, eb=eb: e.transpose(out=pt[:, eb * 128:(eb + 1) * 128],
                                                                                   in_=ob[:, eb * 128:(eb + 1) * 128],
                                                                                   identity=self.cb[:, :]),
                                         reads=["a_ob", "cb_sb"], writes=[ptk])
                                q0 = t0 + qi * 128
                                for eb in range(2):
                                    self.cp(yb[:, eb, q0:q0 + 128], pt[:, eb * 128:(eb + 1) * 128], [ptk], ["a_yb"], eng="act")
                for eb in range(2):
                    self.store(self.d_y[12 + h * 2 + eb, :, :], yb[:, eb, :], "a_yb")
            S.flush()

    def stage_merge(self, l, last):
        S = self.S
        chunks = CHUNKS[1:] if last else CHUNKS
        with ExitStack() as es:
            yk = self.sb(es, "m_y", [128, 36, 512], BF16)
            wt = [self.sb(es, "m_w%d" % i, [128, 36, 128], BF16) for i in range(2)]
            gt = [self.sb(es, "m_g%d" % i, [128, 3, 512], F32) for i in range(2)]
            acc = [self.sb(es, "m_a%d" % i, [128, 512], F32) for i in range(2)]
            tmp = [self.sb(es, "m_t%d" % i, [128, 512], F32) for i in range(2)]
            mo = [self.sb(es, "m_o%d" % i, [128, 512], BF16) for i in range(2)]
            pm = [self.ps(es, "m_p%d" % i) for i in range(6)]
            it = 0
            for (t0, tn) in chunks:
                for cbk in range(36):
                    self.load(yk[:, cbk, 0:tn], self.d_y[cbk, :, t0:t0 + tn], "m_y")
                for dt_ in range(KC):
                    b = dt_ % 2
                    self.load(wt[b][:, :, :], self.c_wbr[dt_], "m_w%d" % b)
                    for k in range(3):
                        self.load(gt[b][:, k, 0:tn], self.d_g[k * KC + dt_, :, t0:t0 + tn], "m_g%d" % b)
                    for k in range(3):
                        p = pm[it % 6]
                        pk = "m_p%d" % (it % 6)
                        it += 1
                        for cc in range(12):
                            self.mm(p[:, 0:tn], wt[b][:, k * 12 + cc, :], yk[:, k * 12 + cc, 0:tn], cc == 0, cc == 11,
                                    ["m_w%d" % b, "m_y"], pk)
                        if k == 0:
                            self.tt(acc[b][:, 0:tn], p[:, 0:tn], gt[b][:, 0, 0:tn], ALU.mult, [pk, "m_g%d" % b], ["m_a%d" % b])
                        else:
                            self.tt(tmp[b][:, 0:tn], p[:, 0:tn], gt[b][:, k, 0:tn], ALU.mult, [pk, "m_g%d" % b], ["m_t%d" % b])
                            if k == 1:
                                self.tt(acc[b][:, 0:tn], acc[b][:, 0:tn], tmp[b][:, 0:tn], ALU.add, ["m_a%d" % b, "m_t%d" % b],
                                        ["m_a%d" % b], eng="pool")
                            else:
                                self.tt(mo[b][:, 0:tn], acc[b][:, 0:tn], tmp[b][:, 0:tn], ALU.add, ["m_a%d" % b, "m_t%d" % b],
                                        ["m_o%d" % b], eng="pool")
                    self.store(self.d_m[dt_, :, t0:t0 + tn], mo[b][:, 0:tn], "m_o%d" % b)
            S.flush()

    def stage_proj_ln(self, l, last, which):
        S = self.S
        chunks = CHUNKS[1:] if last else CHUNKS
        if which == 2:
            chunks = [(t0 + i * 256, 256) for (t0, tn) in chunks for i in range(tn // 256)]
        CW = 512 if which == 1 else 256
        gidx = 2 if which == 1 else 5
        vg, vb = (V_LN1G, V_LN1B) if which == 1 else (V_LN2G, V_LN2B)
        src = self.xsrc(l) if which == 1 else self.d_xs
        with ExitStack() as es:
            xin = self.sb(es, "l_in", [128, KC, CW], BF16)
            rt = self.sb(es, "l_r", [128, KC, CW], F32)
            xt = [self.sb(es, "l_x%d" % i, [128, CW], F32) for i in range(2)]
            tq = [self.sb(es, "l_t%d" % i, [128, CW], F32) for i in range(2)]
            sq = [self.sb(es, "l_q%d" % i, [128, CW], F32) for i in range(2)]
            mean = self.sb(es, "l_mean", [128, CW], F32)
            rstd = self.sb(es, "l_rstd", [128, CW], F32)
            xo = [self.sb(es, "l_xo%d" % i, [128, CW], F32) for i in range(2)]
            uo = [self.sb(es, "l_uo%d" % i, [128, CW], F32) for i in range(2)]
            ub = [self.sb(es, "l_ub%d" % i, [128, CW], BF16) for i in range(2)]
            pm = [self.ps(es, "l_p%d" % i) for i in range(2)]
            pss = self.ps(es, "l_ps")
            psq = self.ps(es, "l_pq")
            if which == 1:
                wt = [self.sb(es, "l_w%d" % i, [128, KC, 128], BF16) for i in range(2)]
            else:
                wt = [self.sb(es, "l_w%d" % i, [128, 48, 128], BF16) for i in range(2)]
                w13 = [self.sb(es, "l_e%d" % i, [128, 2, KC, 128], BF16) for i in range(2)]
                hT = self.sb(es, "l_h", [128, 48, CW], BF16)
                gsb = self.sb(es, "l_G", [128, CW], F32)
                gbs = [self.sb(es, "l_gb%d" % i, [128, CW], F32) for i in range(2)]
                s1 = [self.sb(es, "l_s%d" % i, [128, CW], F32) for i in range(2)]
                ph = [self.ps(es, "l_ph%d" % i) for i in range(2)]
                pg = self.ps(es, "l_pg")
            it = 0
            if which == 2:
                S.op("pool", lambda e: e.memset(gsb[:, :], 0.0), writes=["l_G"])
            for ci, (t0, tn) in enumerate(chunks):
                mod = self.modCc if t0 < CTX else self.modLc
                mk = "modC" if t0 < CTX else "modL"
                if which == 1:
                    for kc in range(KC):
                        self.load(xin[:, kc, 0:tn], self.d_m[kc, :, t0:t0 + tn], "l_in")
                else:
                    for kc in range(KC):
                        self.load(xin[:, kc, 0:tn], self.d_u2[kc, :, t0:t0 + tn], "l_in")
                    self.load(gsb[0:16, 0:tn], self.d_G[:, t0:t0 + tn], "l_G")
                    ih = 0
                    for e_ in range(NE):
                        gb = gbs[e_ % 2]
                        gk = "l_gb%d" % (e_ % 2)
                        self.mm(pg[:, 0:tn], self.sel[:, e_ * 128:(e_ + 1) * 128], gsb[:, 0:tn], True, True, ["sel_sb", "l_G"], "l_pg")
                        self.cp(gb[:, 0:tn], pg[:, 0:tn], ["l_pg"], [gk], eng="act")
                        for ft in range(3):
                            wb_ = w13[ih % 2]
                            wk = "l_e%d" % (ih % 2)
                            ih += 1
                            self.load(wb_[:, 0, :, :], self.c_w1[e_, ft], wk)
                            self.load(wb_[:, 1, :, :], self.c_w3[e_, ft], wk)
                            for wi in range(2):
                                for kc in range(KC):
                                    self.mm(ph[wi][:, 0:tn], wb_[:, wi, kc, :], xin[:, kc, 0:tn], kc == 0, kc == KC - 1,
                                            [wk, "l_in"], "l_ph%d" % wi)
                            sb_ = s1[ih % 2]
                            sk = "l_s%d" % (ih % 2)
                            self.act(sb_[:, 0:tn], ph[0][:, 0:tn], AF.Silu, ["l_ph0"], [sk])
                            self.tt(sb_[:, 0:tn], sb_[:, 0:tn], ph[1][:, 0:tn], ALU.mult, [sk, "l_ph1"], [sk])
                            self.tt(hT[:, e_ * 3 + ft, 0:tn], sb_[:, 0:tn], gb[:, 0:tn], ALU.mult, [sk, gk], ["l_h"], eng="pool")
                for dt_ in range(KC):
                    b = dt_ % 2
                    p = pm[b]
                    pk = "l_p%d" % b
                    if which == 1:
                        self.load(wt[b][:, :, :], self.c_wout[dt_], "l_w%d" % b)
                        for kc in range(KC):
                            self.mm(p[:, 0:tn], wt[b][:, kc, :], xin[:, kc, 0:tn], kc == 0, kc == KC - 1, ["l_w%d" % b, "l_in"], pk)
                    else:
                        self.load(wt[b][:, :, :], self.c_w2[dt_], "l_w%d" % b)
                        for ef in range(48):
                            self.mm(p[:, 0:tn], wt[b][:, ef, :], hT[:, ef, 0:tn], ef == 0, ef == 47, ["l_w%d" % b, "l_h"], pk)
                    self.load(xt[b][:, 0:tn], src[dt_, :, t0:t0 + tn], "l_x%d" % b)
                    self.ts(tq[b][:, 0:tn], p[:, 0:tn], mod[:, gidx, dt_:dt_ + 1], None, ALU.mult, None, [pk, mk], ["l_t%d" % b])
                    self.stt(rt[:, dt_, 0:tn], xt[b][:, 0:tn], float(ALPHA), tq[b][:, 0:tn], ALU.mult, ALU.add,
                             ["l_x%d" % b, "l_t%d" % b], ["l_r"])
                    self.act(sq[b][:, 0:tn], rt[:, dt_, 0:tn], AF.Square, ["l_r"], ["l_q%d" % b])
                    self.mm(pss[:, 0:tn], self.cf[:, 256:384], rt[:, dt_, 0:tn], dt_ == 0, dt_ == KC - 1, ["cf_sb", "l_r"], "l_ps")
                    self.mm(psq[:, 0:tn], self.cf[:, 256:384], sq[b][:, 0:tn], dt_ == 0, dt_ == KC - 1, ["cf_sb", "l_q%d" % b], "l_pq")
                self.ts(mean[:, 0:tn], pss[:, 0:tn], 1.0 / D, None, ALU.mult, None, ["l_ps"], ["l_mean"])
                self.tt(rstd[:, 0:tn], mean[:, 0:tn], mean[:, 0:tn], ALU.mult, ["l_mean"], ["l_rstd"])
                self.stt(rstd[:, 0:tn], psq[:, 0:tn], 1.0 / D, rstd[:, 0:tn], ALU.mult, ALU.subtract, ["l_pq", "l_rstd"], ["l_rstd"])
                self.ts(rstd[:, 0:tn], rstd[:, 0:tn], EPS, None, ALU.add, None, ["l_rstd"], ["l_rstd"])
                self.act(rstd[:, 0:tn], rstd[:, 0:tn], AF.Sqrt, ["l_rstd"], ["l_rstd"])
                S.op("dve", lambda e, tn=tn: e.reciprocal(out=rstd[:, 0:tn], in_=rstd[:, 0:tn]), reads=["l_rstd"], writes=["l_rstd"])
                for dt_ in range(KC):
                    b = dt_ % 2
                    self.tt(tq[b][:, 0:tn], rt[:, dt_, 0:tn], mean[:, 0:tn], ALU.subtract, ["l_r", "l_mean"], ["l_t%d" % b])
                    self.tt(tq[b][:, 0:tn], tq[b][:, 0:tn], rstd[:, 0:tn], ALU.mult, ["l_t%d" % b, "l_rstd"], ["l_t%d" % b], eng="pool")
                    self.ts(xo[b][:, 0:tn], tq[b][:, 0:tn], self.vecsc[:, vg + dt_:vg + dt_ + 1], self.vecsc[:, vb + dt_:vb + dt_ + 1],
                            ALU.mult, ALU.add, ["l_t%d" % b, "vecs_sb"], ["l_xo%d" % b])
                    final = False
                    if final and t0 >= CTX:
                        self.store(self.o_out[dt_, :, t0 - CTX:t0 - CTX + tn], xo[b][:, 0:tn], "l_xo%d" % b)
                    else:
                        self.store(self.d_xs[dt_, :, t0:t0 + tn], xo[b][:, 0:tn], "l_xo%d" % b)
                    if which == 1:
                        self.ts(uo[b][:, 0:tn], xo[b][:, 0:tn], mod[:, 4, dt_:dt_ + 1], mod[:, 3, dt_:dt_ + 1], ALU.mult, ALU.add,
                                ["l_xo%d" % b, mk], ["l_uo%d" % b])
                        self.cp(ub[b][:, 0:tn], uo[b][:, 0:tn], ["l_uo%d" % b], ["l_ub%d" % b], eng="act")
                        self.store(self.d_u2f[dt_, :, t0:t0 + tn], uo[b][:, 0:tn], "l_uo%d" % b)
                        self.store(self.d_u2[dt_, :, t0:t0 + tn], ub[b][:, 0:tn], "l_ub%d" % b)
            S.flush()

    def stage_router(self, l, last):
        S = self.S
        import os
        RD = os.environ.get("KDBG", "")
        segs = [(CTX, SEQ, 256)] if last else [(0, CTX, 32), (CTX, SEQ, 256)]
        chunks = CHUNKS[1:] if last else CHUNKS
        with ExitStack() as es:
            rw = self.sb(es, "t_rw", [128, KC, 128], F32)
            uf = [self.sb(es, "t_u%d" % i, [128, 512], F32) for i in range(3)]
            aff = self.sb(es, "t_aff", [128, T], F32)
            wk_ = self.sb(es, "t_wk", [16, T], F32)
            rs = self.sb(es, "t_rs", [16, 512], F32)
            m8 = self.sb(es, "t_m8", [16, 8], F32)
            G = self.sb(es, "t_G", [16, T], F32)
            pl_ = self.ps(es, "t_pl")
            psm = self.ps(es, "t_psm")
            self.load(rw[:, :, :], self.i_router[l].rearrange("(kc p) e -> p kc e", p=128), "t_rw")
            S.op("pool", lambda e: e.memset(G[:, :], 0.0), writes=["t_G"])
            S.op("pool", lambda e: e.memset(aff[:, :], 0.0), writes=["t_aff"])
            iu = 0
            for (t0, tn) in chunks:
                for kc in range(KC):
                    u_ = uf[iu % 3]
                    uk = "t_u%d" % (iu % 3)
                    iu += 1
                    self.load(u_[:, 0:tn], self.d_u2f[kc, :, t0:t0 + tn], uk)
                    self.mm(pl_[:, 0:tn], rw[:, kc, :], u_[:, 0:tn], kc == 0, kc == KC - 1, ["t_rw", uk], "t_pl", inc=True)
                self.act(aff[0:16, t0:t0 + tn], pl_[0:16, 0:tn], AF.Exp, ["t_pl"], ["t_aff"])
                if "R1" in RD:
                    continue
                self.mm(psm[:, 0:tn], self.cf[:, 256:384], aff[:, t0:t0 + tn], True, True, ["cf_sb", "t_aff"], "t_psm")
                S.op("dve", lambda e, tn=tn: e.reciprocal(out=rs[:, 0:tn], in_=psm[0:16, 0:tn]), reads=["t_psm"], writes=["t_rs"])
                self.tt(aff[0:16, t0:t0 + tn], aff[0:16, t0:t0 + tn], rs[:, 0:tn], ALU.mult, ["t_aff", "t_rs"], ["t_aff"])
            for (s0, sn, cap) in segs:
                if "R1" in RD or "R2" in RD:
                    break
                self.cp(wk_[:, s0:s0 + sn], aff[0:16, s0:s0 + sn], ["t_aff"], ["t_wk"])
                for r in range(cap // 8):
                    S.op("dve", lambda e, s0=s0, sn=sn: e.max(out=m8[:, :], in_=wk_[:, s0:s0 + sn]), reads=["t_wk"], writes=["t_m8"])
                    if r < cap // 8 - 1:
                        S.op("dve", lambda e, s0=s0, sn=sn: e.match_replace(out=wk_[:, s0:s0 + sn], in_to_replace=m8[:, :],
                                                                             in_values=wk_[:, s0:s0 + sn], imm_value=-1.0),
                             reads=["t_wk", "t_m8"], writes=["t_wk"])
                self.ts(wk_[:, s0:s0 + sn], aff[0:16, s0:s0 + sn], m8[:, 7:8], None, ALU.is_ge, None, ["t_aff", "t_m8"], ["t_wk"])
                self.tt(G[:, s0:s0 + sn], wk_[:, s0:s0 + sn], aff[0:16, s0:s0 + sn], ALU.mult, ["t_wk", "t_aff"], ["t_G"])
            self.store(self.d_G[:, :], G[:, :], "t_G")
            S.flush()

    def stage_final(self):
        S = self.S
        for i in range(0, KC, 4):
            S.dma("sp", lambda e, i=i: e.dma_start(out=self.o_out[i:i + 4, :, :], in_=self.d_xs[i:i + 4, :, CTX:T]), "fin%d" % (i % 8))
        S.flush()

    def stage_output(self):
        if not self.dump:
            return
        S = self.S
        for name in self.dump:
            src = getattr(self, name)
            shp = list(src.shape)
            o = self.nc.dram_tensor("dump_" + name, shp, src.dtype, kind="ExternalOutput")
            n0 = shp[0]
            step = max(1, n0 // 8)
            for i in range(0, n0, step):
                j = min(n0, i + step)
                S.dma("sp", lambda e, o=o, src=src, i=i, j=j: e.dma_start(out=o[i:j], in_=src[i:j]), "dump")
        S.flush()


def _consts():
    ident = np.eye(128, dtype=np.float32)
    perm = np.zeros((128, 128), np.float32)
    for m in range(128):
        partner = m + 32 if (m % 64) < 32 else m - 32
        perm[partner, m] = 1.0
    ones = np.ones((128, 128), np.float32)
    quarter = 32
    inv_freq = (10000.0 ** (-np.arange(quarter, dtype=np.float32) / quarter)).astype(np.float32)
    n = np.arange(SEQ)
    rows = (n // 64).astype(np.float32)
    cols = (n % 64).astype(np.float32)
    C = np.ones((128, T), np.float32)
    Sg = np.zeros((128, T), np.float32)
    for d in range(128):
        pos = rows if d < 64 else cols
        dd = d % 64
        ang = (pos * inv_freq[dd % 32]).astype(np.float32)
        C[d, CTX:] = np.cos(ang)
        s = np.sin(ang)
        Sg[d, CTX:] = -s if dd < 32 else s
    cf = np.concatenate([ident, perm, ones, C, Sg], axis=1).astype(np.float32)
    cb = np.eye(128, dtype=np.float32).astype(ml_dtypes.bfloat16)
    sel = np.zeros((128, NE * 128), np.float32)
    for e in range(NE):
        sel[e, e * 128:(e + 1) * 128] = 1.0
    rc = np.zeros((4, T), np.float32)
    for g, win in enumerate((2, 4, 8, 16)):
        for (s0, Ls) in ((0, CTX), (CTX, SEQ)):
            t = np.arange(Ls)
            lo = np.clip(t - win // 2, 0, Ls)
            hi = np.clip(t + win // 2, 0, Ls)
            rc[g, s0:s0 + Ls] = 1.0 / (hi - lo).astype(np.float32)
    rc = np.ascontiguousarray(np.broadcast_to(rc[:, None, :], (4, 128, T)))
    return cf, cb, sel, rc


def _pvec(v):
    v = np.asarray(v, np.float32)
    return v.reshape(-1, 128).T


def prepare_inputs(inp, depth, cores):
    f = lambda a: np.ascontiguousarray(np.asarray(a, dtype=np.float32))
    Lr = depth
    cf, cb, sel, rc = _consts()
    vecs = np.zeros((Lr, 128, NV), np.float32)
    dav = np.zeros((Lr, 128, 768), np.float32)
    mbt = np.zeros((Lr, 128, NMOD * KC), np.float32)
    for l in range(Lr):
        vecs[l, :, V_LN1G:V_LN1G + 32] = _pvec(inp["ln1_g"][l])
        vecs[l, :, V_LN1B:V_LN1B + 32] = _pvec(inp["ln1_b"][l])
        vecs[l, :, V_LN2G:V_LN2G + 32] = _pvec(inp["ln2_g"][l])
        vecs[l, :, V_LN2B:V_LN2B + 32] = _pvec(inp["ln2_b"][l])
        for k in range(4):
            vecs[l, :, V_CW + k * 12:V_CW + (k + 1) * 12] = _pvec(inp["conv_w"][l, k])
        vecs[l, :, V_CB:V_CB + 12] = _pvec(inp["conv_b"][l])
        for d in range(2):
            vecs[l, :, V_BA + d * 12:V_BA + (d + 1) * 12] = _pvec(inp["rg_ba"][l, d])
            vecs[l, :, V_BX + d * 12:V_BX + (d + 1) * 12] = _pvec(inp["rg_bx"][l, d])
            vecs[l, :, V_LAM + d * 12:V_LAM + (d + 1) * 12] = _pvec(inp["rg_lambda"][l, d])
        vecs[l, :, V_PS:V_PS + 12] = _pvec(inp["pool_scale"][l])
        for i, nm in enumerate(("da_lq1", "da_lk1", "da_lq2", "da_lk2")):
            dav[l, :, i * 128:(i + 1) * 128] = np.asarray(inp[nm][l], np.float32)[None, :]
        dav[l, :, 512:768] = np.asarray(inp["da_norm"][l], np.float32)[None, :]
        mbt[l] = _pvec(inp["mod_bias"][l])
    rgw = np.ascontiguousarray(np.concatenate([np.asarray(inp["rg_wa"][:Lr], np.float32), np.asarray(inp["rg_wx"][:Lr], np.float32)], axis=1))
    shared = {
        "mod_a": f(inp["mod_a"][:Lr]), "mod_b": f(inp["mod_b"][:Lr]), "mod_bias_t": mbt, "vecs": vecs, "dav": dav,
        "w_in": np.ascontiguousarray(f(inp["w_in"][:Lr]).reshape(Lr, KC, 128, INC // 128, 128).transpose(0, 3, 2, 1, 4)), "rg_w": rgw, "pool_w": f(inp["pool_w"][:Lr]),
        "w_branch": np.ascontiguousarray(f(inp["w_branch"][:Lr]).reshape(Lr, 36, 128, KC, 128).transpose(0, 3, 2, 1, 4)),
        "w_out": np.ascontiguousarray(f(inp["w_out"][:Lr]).reshape(Lr, KC, 128, KC, 128).transpose(0, 3, 2, 1, 4)),
        "router": np.ascontiguousarray(np.pad(f(inp["router"][:Lr]), ((0, 0), (0, 0), (0, 128 - NE)))), "ex_w1": np.ascontiguousarray(f(inp["ex_w1"][:Lr]).reshape(Lr, NE, KC, 128, 3, 128).transpose(0, 1, 4, 3, 2, 5)),
        "ex_w3": np.ascontiguousarray(f(inp["ex_w3"][:Lr]).reshape(Lr, NE, KC, 128, 3, 128).transpose(0, 1, 4, 3, 2, 5)),
        "ex_w2": np.ascontiguousarray(f(inp["ex_w2"][:Lr]).reshape(Lr, 48, 128, KC, 128).transpose(0, 3, 2, 1, 4)),
        "const_f": cf, "const_b": cb, "const_sel": sel, "const_rc": rc,
    }
    maps = []
    cctx = np.asarray(inp["c_ctx"], np.float32)
    for b in cores:
        xT = np.concatenate([np.asarray(inp["ctx"][b], np.float32).T, np.asarray(inp["x"][b], np.float32).T], axis=1)
        cs = np.stack([_pvec(inp["c"][b]), _pvec(cctx)], axis=-1)
        m = dict(shared)
        m["xT"] = np.ascontiguousarray(xT.reshape(KC, 128, T))
        m["cs"] = np.ascontiguousarray(cs)
        maps.append(m)
    return maps


def kernel(**inputs):
    prog = K(L_FULL)
    nc = prog.build()
    maps = prepare_inputs(inputs, L_FULL, list(range(NB)))
    res = run_bass_kernel_spmd(nc, maps, core_ids=list(range(NB)))
    out = np.stack([np.asarray(r["outT"], np.float32).reshape(D, SEQ).T for r in res.results], axis=0)
    return np.ascontiguousarray(out)
```

```python
import math
from contextlib import ExitStack

import numpy as np
import ml_dtypes
import concourse.bass as bass
import concourse.mybir as mybir
from concourse.bass_utils import run_bass_kernel_spmd

F32 = mybir.dt.float32
BF16 = mybir.dt.bfloat16
AF = mybir.ActivationFunctionType
ALU = mybir.AluOpType

D = 4096
NB = 8
SEQ = 2048
CTX = 256
T = CTX + SEQ
L_FULL = 4
USE_FORI = False
BW = 1536
INC = 21504
NMOD = 6
NE = 16
FF = 384
KC = D // 128
ALPHA = (2 * L_FULL) ** 0.25
EPS = 1e-5
CHUNKS = [(0, 256), (256, 512), (768, 512), (1280, 512), (1792, 512)]
NV = 272
V_LN1G, V_LN1B, V_LN2G, V_LN2B, V_CW, V_CB, V_BA, V_BX, V_LAM, V_PS = 0, 32, 64, 96, 128, 176, 188, 212, 236, 260


class Sched:
    CE = ["pe", "act", "dve", "pool"]
    ALL = ["pe", "act", "dve", "pool", "sp"]

    def __init__(self, nc, es, ndma=40):
        self.nc = nc
        self.sem = {e: es.enter_context(nc.semaphore("s_" + e)) for e in self.CE}
        self.rel = [es.enter_context(nc.semaphore("s_rel%d" % i)) for i in range(3)]
        self.dsem = [es.enter_context(nc.semaphore("d%d" % i)) for i in range(ndma)]
        self.nbar = 0
        self.nstage = 0
        self._reset()

    def _reset(self):
        self.ops = {e: [] for e in self.ALL}
        self.cnt = {e: 0 for e in self.CE}
        self.dcnt = {}
        self.dkey = {}
        self.known = {e: {} for e in self.ALL}
        self.last_w = {}
        self.readers = {}

    def _deps(self, eng, reads, writes):
        deps = {}

        def add(tok):
            if tok is None:
                return
            sid, val = tok
            if deps.get(sid, 0) < val:
                deps[sid] = val

        for k in reads:
            add(self.last_w.get(k))
        for k in writes:
            add(self.last_w.get(k))
            for sid, val in self.readers.get(k, {}).items():
                add((sid, val))
        waits = []
        for sid, val in deps.items():
            if eng == "pe" and sid == ("e", "pe"):
                continue
            if self.known[eng].get(sid, 0) >= val:
                continue
            self.known[eng][sid] = val
            waits.append((sid, val))
        return waits

    def _mark(self, tok, reads, writes):
        for k in writes:
            self.last_w[k] = tok
            self.readers[k] = {}
        for k in reads:
            if k in writes:
                continue
            r = self.readers.setdefault(k, {})
            if r.get(tok[0], 0) < tok[1]:
                r[tok[0]] = tok[1]

    def op(self, eng, fn, reads=(), writes=(), inc=True):
        waits = self._deps(eng, reads, writes)
        if inc:
            self.cnt[eng] += 1
            tok = (("e", eng), self.cnt[eng])
        else:
            tok = (("e", eng), self.cnt[eng] + 1)
        self.ops[eng].append((waits, fn, ("e", eng) if inc else None))
        self._mark(tok, reads, writes)

    def dma(self, q, fn, key, reads=(), writes=()):
        waits = self._deps(q, reads, writes)
        if key not in self.dkey:
            self.dkey[key] = len(self.dkey)
            assert len(self.dkey) <= len(self.dsem), "out of DMA semaphores"
        i = self.dkey[key]
        self.dcnt[i] = self.dcnt.get(i, 0) + 16
        tok = (("d", i), self.dcnt[i])
        self.ops[q].append((waits, fn, ("d", i)))
        self._mark(tok, reads, writes)

    def _semof(self, sid):
        return self.sem[sid[1]] if sid[0] == "e" else self.dsem[sid[1]]

    def flush(self):
        nc = self.nc
        kb = self.nbar % 3
        self.nbar += 1
        used = [("e", e) for e in self.CE if self.cnt[e] > 0] + [("d", i) for i in self.dcnt]
        totals = {("e", e): self.cnt[e] for e in self.CE}
        totals.update({("d", i): v for i, v in self.dcnt.items()})
        engobj = {"pe": nc.tensor, "act": nc.scalar, "dve": nc.vector, "pool": nc.gpsimd, "sp": nc.sync}

        def body(ename):
            def f(eng):
                for waits, fn, incspec in self.ops[ename]:
                    for sid, val in waits:
                        eng.wait_ge(self._semof(sid), val)
                    ins = fn(eng)
                    if incspec is not None:
                        ins.then_inc(self._semof(incspec), 1 if incspec[0] == "e" else 16)
                if ename == "pool":
                    for sid in used:
                        eng.wait_ge(self._semof(sid), totals[sid])
                    import os
                    if "noclear" not in os.environ.get("KDBG", ""):
                        for sid in used:
                            eng.sem_clear(self._semof(sid))
                    eng.sem_clear(self.rel[(kb + 1) % 3])
                    eng.sem_inc(self.rel[kb], 1)
                else:
                    eng.wait_ge(self.rel[kb], 1)
            return f

        with nc.Block() as block:
            block.tensor(body("pe"))
            block.scalar(body("act"))
            block.vector(body("dve"))
            block.gpsimd(body("pool"))
            block.sync(body("sp"))
        self.nstage += 1
        self._reset()


class K:
    def __init__(self, depth, dump=(), stop_after=None):
        self.depth = depth
        self.dump = dump
        self.stop_after = stop_after
        self.nc = bass.Bass("TRN2", target_bir_lowering=False)
        self.ges = ExitStack()
        self.S = None

    def dram_in(self, name, shape, dt=F32):
        return self.nc.dram_tensor(name, list(shape), dt, kind="ExternalInput")

    def dram(self, name, shape, dt):
        return self.nc.dram_tensor(name, list(shape), dt)

    def sb(self, es, name, shape, dt):
        self._uid = getattr(self, "_uid", 0) + 1
        return es.enter_context(self.nc.sbuf_tensor("%s_%d" % (name, self._uid), list(shape), dt))

    def ps(self, es, name, shape=(128, 512), dt=F32):
        self._uid = getattr(self, "_uid", 0) + 1
        return es.enter_context(self.nc.psum_tensor("%s_%d" % (name, self._uid), list(shape), dt))

    def load(self, dst_ap, src_ap, key, q="sp", **kw):
        self.S.dma(q, lambda e: e.dma_start(out=dst_ap, in_=src_ap, **kw), key, writes=[key])

    def loadw(self, dst, src, n_mid, key, parts=4):
        step = (n_mid + parts - 1) // parts
        for m0 in range(0, n_mid, step):
            m1 = min(n_mid, m0 + step)
            self.load(dst[:, m0:m1, :], src[:, m0:m1, :], key)

    def store(self, dst_ap, src_ap, key, q="act", **kw):
        self.S.dma(q, lambda e: e.dma_start(out=dst_ap, in_=src_ap, **kw), key, reads=[key])

    def mm(self, out, lhsT, rhs, start, stop, reads, pkey, inc=None):
        self.S.op("pe", lambda e: e.matmul(out, lhsT=lhsT, rhs=rhs, start=start, stop=stop),
                  reads=reads, writes=[pkey], inc=(stop if inc is None else inc))

    def act(self, out, in_, func, reads, writes, **kw):
        self.S.op("act", lambda e: e.activation(out=out, in_=in_, func=func, **kw), reads=reads, writes=writes)

    def ts(self, out, in0, s1, s2, op0, op1, reads, writes, eng="dve"):
        if op1 is None:
            self.S.op(eng, lambda e: e.tensor_scalar(out=out, in0=in0, scalar1=s1, scalar2=None, op0=op0),
                      reads=reads, writes=writes)
        else:
            self.S.op(eng, lambda e: e.tensor_scalar(out=out, in0=in0, scalar1=s1, scalar2=s2, op0=op0, op1=op1),
                      reads=reads, writes=writes)

    def tt(self, out, in0, in1, op, reads, writes, eng="dve"):
        self.S.op(eng, lambda e: e.tensor_tensor(out=out, in0=in0, in1=in1, op=op), reads=reads, writes=writes)

    def stt(self, out, in0, scalar, in1, op0, op1, reads, writes):
        self.S.op("dve", lambda e: e.scalar_tensor_tensor(out=out, in0=in0, scalar=scalar, in1=in1, op0=op0, op1=op1),
                  reads=reads, writes=writes)

    def cp(self, out, in_, reads, writes, eng="dve"):
        if eng == "act":
            self.S.op("act", lambda e: e.copy(out=out, in_=in_), reads=reads, writes=writes)
        else:
            self.S.op(eng, lambda e: e.tensor_copy(out=out, in_=in_), reads=reads, writes=writes)

    def build(self):
        nc = self.nc
        Lr = self.depth
        ges = self.ges
        self.i_x = self.dram_in("xT", [KC, 128, T])
        self.i_cs = self.dram_in("cs", [128, KC, 2])
        self.i_moda = self.dram_in("mod_a", [Lr, D, 256])
        self.i_modb = self.dram_in("mod_b", [Lr, 256, NMOD * D])
        self.i_modbias = self.dram_in("mod_bias_t", [Lr, 128, NMOD * KC])
        self.i_vecs = self.dram_in("vecs", [Lr, 128, NV])
        self.i_dav = self.dram_in("dav", [Lr, 128, 768])
        self.i_win = self.dram_in("w_in", [Lr, INC // 128, 128, KC, 128])
        self.i_rgw = self.dram_in("rg_w", [Lr, 4, 12, 128, 128])
        self.i_poolw = self.dram_in("pool_w", [Lr, 4, 384, 384])
        self.i_wbr = self.dram_in("w_branch", [Lr, KC, 128, 36, 128])
        self.i_wout = self.dram_in("w_out", [Lr, KC, 128, KC, 128])
        self.i_router = self.dram_in("router", [Lr, D, 128])
        self.i_w1 = self.dram_in("ex_w1", [Lr, NE, 3, 128, KC, 128])
        self.i_w3 = self.dram_in("ex_w3", [Lr, NE, 3, 128, KC, 128])
        self.i_w2 = self.dram_in("ex_w2", [Lr, KC, 128, 48, 128])
        self.i_cf = self.dram_in("const_f", [128, 128 * 3 + 2 * T])
        self.i_cb = self.dram_in("const_b", [128, 128], BF16)
        self.i_sel = self.dram_in("const_sel", [128, NE * 128])
        self.i_rc = self.dram_in("const_rc", [4, 128, T])
        self.o_out = nc.dram_tensor("outT", [KC, 128, SEQ], F32, kind="ExternalOutput")
        self.b_win = [self.dram("b_win_p%d" % p_, [Lr, 56, 128, KC, 128], BF16).ap() for p_ in range(3)]
        self.b_rgw = self.dram("b_rgw", [Lr] + [4, 12, 128, 128], BF16).ap()
        self.b_poolw = self.dram("b_poolw", [Lr] + [4, 384, 384], BF16).ap()
        self.b_wbr = self.dram("b_wbr", [Lr, KC, 128, 36, 128], BF16).ap()
        self.b_wout = self.dram("b_wout", [Lr, KC, 128, KC, 128], BF16).ap()
        self.b_w1 = self.dram("b_w1", [Lr, NE, 3, 128, KC, 128], BF16).ap()
        self.b_w3 = self.dram("b_w3", [Lr, NE, 3, 128, KC, 128], BF16).ap()
        self.b_w2 = self.dram("b_w2", [Lr, KC, 128, 48, 128], BF16).ap()
        self.c_win = [self.dram("c_win_p%d" % p_, [56, 128, KC, 128], BF16).ap() for p_ in range(3)]
        self.c_rgw = self.dram("c_rgw", [4, 12, 128, 128], BF16).ap()
        self.c_poolw = self.dram("c_poolw", [4, 384, 384], BF16).ap()
        self.c_wbr = self.dram("c_wbr", [KC, 128, 36, 128], BF16).ap()
        self.c_wout = self.dram("c_wout", [KC, 128, KC, 128], BF16).ap()
        self.c_w1 = self.dram("c_w1", [NE, 3, 128, KC, 128], BF16).ap()
        self.c_w3 = self.dram("c_w3", [NE, 3, 128, KC, 128], BF16).ap()
        self.c_w2 = self.dram("c_w2", [KC, 128, 48, 128], BF16).ap()
        self.d_xs = self.dram("d_xs", [KC, 128, T], F32)
        self.d_u = self.dram("d_u", [KC, 128, T], BF16)
        self.d_xa = self.dram("d_xa", [12, 128, T], F32)
        self.d_ga = self.dram("d_ga", [12, 128, T], F32)
        self.d_qk = self.dram("d_qk", [24, 128, T], BF16)
        self.d_v = self.dram("d_v", [12, 128, T], BF16)
        self.d_pl = self.dram("d_pl", [12, 128, T], F32)
        self.d_g = self.dram("d_g", [96, 128, T], F32)
        self.d_y = self.dram("d_y", [36, 128, T], BF16)
        self.d_m = self.dram("d_m", [KC, 128, T], BF16)
        self.d_u2 = self.dram("d_u2", [KC, 128, T], BF16)
        self.d_u2f = self.dram("d_u2f", [KC, 128, T], F32)
        self.d_G = self.dram("d_G", [NE, T], F32)
        self.dumps = {}

        self.S = Sched(nc, ges)
        self.modL = self.sb(ges, "modL", [128, Lr, NMOD, KC], F32)
        self.modC = self.sb(ges, "modC", [128, Lr, NMOD, KC], F32)
        self.vecs = self.sb(ges, "vecs_sb", [128, Lr, NV], F32)
        self.dnorm = self.sb(ges, "dnorm", [128, Lr, 256], F32)
        self.lamv = self.sb(ges, "lamv", [128, Lr, 2], F32)
        self.c8 = self.sb(ges, "c8", [128, Lr, 24], F32)
        self.modLc = self.sb(ges, "modLc", [128, NMOD, KC], F32)
        self.modCc = self.sb(ges, "modCc", [128, NMOD, KC], F32)
        self.vecsc = self.sb(ges, "vecsc", [128, NV], F32)
        self.dnormc = self.sb(ges, "dnormc", [128, 256], F32)
        self.lamvc = self.sb(ges, "lamvc", [128, 2], F32)
        self.c8c = self.sb(ges, "c8c", [128, 24], F32)
        self.d_modL = self.dram("d_modL", [Lr, 128, NMOD * KC], F32)
        self.d_modC = self.dram("d_modC", [Lr, 128, NMOD * KC], F32)
        self.d_dnorm = self.dram("d_dnorm", [Lr, 128, 256], F32)
        self.d_lamv = self.dram("d_lamv", [Lr, 128, 2], F32)
        self.d_c8 = self.dram("d_c8", [Lr, 128, 24], F32)
        self.cf = self.sb(ges, "cf_sb", [128, 384], F32)
        self.cb = self.sb(ges, "cb_sb", [128, 128], BF16)
        self.sel = self.sb(ges, "sel_sb", [128, NE * 128], F32)

        import os
        dbg = os.environ.get("KDBG", "")
        if "nosetup" not in dbg:
            self.stage_setup()
        if "noprecast" not in dbg:
            self.stage_precast()
        stages = ["modulate", "win", "rglru", "pool", "attn", "merge", "ln1", "router", "ln2"]
        done = "nolayers" in dbg
        if USE_FORI:
            loop_cm = nc.Fori(0, Lr)
            layer_ids = [loop_cm.__enter__()]
        else:
            loop_cm = None
            layer_ids = list(range(Lr))
        for l in layer_ids:
            last = (not USE_FORI) and (l == L_FULL - 1)
            if not USE_FORI:
                self.c_win = [self.b_win[p_][l] for p_ in range(3)]
                self.c_rgw, self.c_poolw, self.c_wbr, self.c_wout = self.b_rgw[l], self.b_poolw[l], self.b_wbr[l], self.b_wout[l]
                self.c_w1, self.c_w3, self.c_w2 = self.b_w1[l], self.b_w3[l], self.b_w2[l]
            for st in stages:
                if done:
                    break
                if "only=" in dbg and ("only=" + st) not in dbg:
                    if self.stop_after is not None and self.stop_after[1] == st:
                        done = True
                    continue
                if st == "modulate":
                    self.stage_modulate(l)
                elif st == "win":
                    self.stage_win(l)
                elif st == "rglru":
                    self.stage_rglru(l)
                elif st == "pool":
                    self.stage_pool(l)
                elif st == "attn":
                    self.stage_attn(l)
                elif st == "merge":
                    self.stage_merge(l, last)
                elif st == "ln1":
                    self.stage_proj_ln(l, last, which=1)
                elif st == "router":
                    self.stage_router(l, last)
                elif st == "ln2":
                    self.stage_proj_ln(l, last, which=2)
                if self.stop_after is not None and self.stop_after[1] == st and (USE_FORI or self.stop_after[0] == l):
                    done = True
        if loop_cm is not None:
            loop_cm.__exit__(None, None, None)
        self.stage_final()
        self.stage_output()
        ges.close()
        return nc

    def lambda_init(self, l):
        return 0.8 - 0.6 * math.exp(-0.3 * l)

    def stage_setup(self):
        S = self.S
        Lr = self.depth
        with ExitStack() as es:
            cs = self.sb(es, "cs_sb", [128, KC, 2], F32)
            ss = self.sb(es, "ss_sb", [128, KC, 2], F32)
            ma = self.sb(es, "ma_sb", [128, KC, 256], F32)
            mb = self.sb(es, "mb_sb", [128, 2, D], F32)
            mbias = self.sb(es, "mbias_sb", [128, NMOD * KC], F32)
            hs = self.sb(es, "hs_sb", [128, 4], F32)
            dav = self.sb(es, "dav_sb", [128, 768], F32)
            tmp = self.sb(es, "tmp_sb", [128, 128], F32)
            red = self.sb(es, "red_sb", [128, 4], F32)
            xe = self.sb(es, "xe_sb", [128, 24], F32)
            pl = self.sb(es, "pl_sb", [128, 24], F32)
            p1 = self.ps(es, "p1", [128, 512])
            p2 = self.ps(es, "p2", [128, 512])
            self.load(self.cf[:, :], self.i_cf[:, 0:384], "cf_sb")
            self.load(self.cb[:, :], self.i_cb[:, :], "cb_sb")
            self.load(self.sel[:, :], self.i_sel[:, :], "sel_sb")
            self.load(cs[:, :, :], self.i_cs[:, :, :], "cs_sb")
            for l in range(Lr):
                self.load(self.vecs[:, l, :], self.i_vecs[l, :, :], "vecs_sb")
            self.act(ss[:, :, :], cs[:, :, :], AF.Silu, ["cs_sb"], ["ss_sb"])
            for l in range(Lr):
                self.load(ma[:, :, :], self.i_moda[l].rearrange("(kc p) r -> p kc r", p=128), "ma_sb")
                self.load(mbias[:, :], self.i_modbias[l, :, :], "mbias_sb")
                for rt in range(2):
                    for kc in range(KC):
                        self.mm(p1[:, 2 * rt:2 * rt + 2], ma[:, kc, rt * 128:(rt + 1) * 128], ss[:, kc, :],
                                kc == 0, kc == KC - 1, ["ma_sb", "ss_sb"], "p1")
                self.cp(hs[:, :], p1[:, 0:4], ["p1"], ["hs_sb"])
                for j in range(NMOD):
                    self.load(mb[:, :, :], self.i_modb[l, :, j * D:(j + 1) * D].rearrange("(rc p) f -> p rc f", p=128), "mb_sb")
                    for c in range(KC):
                        for rc in range(2):
                            self.mm(p2[:, 2 * c:2 * c + 2], mb[:, rc, c * 128:(c + 1) * 128], hs[:, 2 * rc:2 * rc + 2],
                                    rc == 0, rc == 1, ["mb_sb", "hs_sb"], "p2")
                    add1 = 1.0 if j in (1, 2, 4, 5) else 0.0
                    for col, dst, dk in ((0, self.modL, "modL"), (1, self.modC, "modC")):
                        self.tt(dst[:, l, j, :], p2[:, col:2 * KC:2], mbias[:, j * KC:(j + 1) * KC], ALU.add,
                                ["p2", "mbias_sb"], [dk])
                        if add1:
                            self.ts(dst[:, l, j, :], dst[:, l, j, :], 1.0, None, ALU.add, None, [dk], [dk])
                self.load(dav[:, :], self.i_dav[l, :, :], "dav_sb")
                for i in range(2):
                    self.tt(tmp[:, :], dav[:, 256 * i:256 * i + 128], dav[:, 256 * i + 128:256 * i + 256], ALU.mult,
                            ["dav_sb"], ["tmp_sb"])
                    S.op("dve", lambda e, i=i: e.reduce_sum(out=red[:, i:i + 1], in_=tmp[:, :], axis=mybir.AxisListType.X),
                         reads=["tmp_sb"], writes=["red_sb"])
                self.act(red[:, 2:4], red[:, 0:2], AF.Exp, ["red_sb"], ["red_sb"])
                self.tt(self.lamv[:, l, 0:1], red[:, 2:3], red[:, 3:4], ALU.subtract, ["red_sb"], ["lamv"])
                self.ts(self.lamv[:, l, 0:1], self.lamv[:, l, 0:1], float(self.lambda_init(l)), None, ALU.add, None, ["lamv"], ["lamv"])
                self.ts(self.lamv[:, l, 1:2], self.lamv[:, l, 0:1], -1.0, None, ALU.mult, None, ["lamv"], ["lamv"])
                self.ts(self.dnorm[:, l, :], dav[:, 512:768], float(1.0 - self.lambda_init(l)), None, ALU.mult, None,
                        ["dav_sb"], ["dnorm"])
                self.act(xe[:, :], self.vecs[:, l, V_LAM:V_LAM + 24], AF.Exp, ["vecs_sb"], ["xe_sb"], scale=-1.0)
                self.ts(pl[:, :], xe[:, :], -0.25, 1.0 / 3.0, ALU.mult, ALU.add, ["xe_sb"], ["pl_sb"])
                self.tt(pl[:, :], pl[:, :], xe[:, :], ALU.mult, ["pl_sb", "xe_sb"], ["pl_sb"])
                self.ts(pl[:, :], pl[:, :], -1.0, 0.5, ALU.mult, ALU.add, ["pl_sb"], ["pl_sb"])
                self.tt(pl[:, :], pl[:, :], xe[:, :], ALU.mult, ["pl_sb", "xe_sb"], ["pl_sb"])
                self.ts(pl[:, :], pl[:, :], -1.0, 1.0, ALU.mult, ALU.add, ["pl_sb"], ["pl_sb"])
                self.tt(pl[:, :], pl[:, :], xe[:, :], ALU.mult, ["pl_sb", "xe_sb"], ["pl_sb"])
                self.ts(self.c8[:, l, :], pl[:, :], -8.0, None, ALU.mult, None, ["pl_sb"], ["c8"])
            for l in range(Lr):
                self.store(self.d_modL[l].rearrange("p (j c) -> p j c", j=NMOD), self.modL[:, l, :, :], "modL")
                self.store(self.d_modC[l].rearrange("p (j c) -> p j c", j=NMOD), self.modC[:, l, :, :], "modC")
                self.store(self.d_dnorm[l], self.dnorm[:, l, :], "dnorm")
                self.store(self.d_lamv[l], self.lamv[:, l, :], "lamv")
                self.store(self.d_c8[l], self.c8[:, l, :], "c8")
            self.S.dma("sp", lambda e: e.dma_start(out=self.d_xs[0:16], in_=self.i_x[0:16]), "xcopy0")
            self.S.dma("sp", lambda e: e.dma_start(out=self.d_xs[16:32], in_=self.i_x[16:32]), "xcopy1")
            S.flush()

    def stage_precast(self):
        S = self.S
        Lr = self.depth
        import os
        with ExitStack() as es:
            fi = [self.sb(es, "pc_f%d" % i, [128, 4096], F32) for i in range(4)]
            bo = [self.sb(es, "pc_b%d" % i, [128, 4096], BF16) for i in range(4)]
            it = 0

            def flat(ap, pat):
                return ap.rearrange(pat).rearrange("(r c) -> r c", c=4096)

            P4 = "a p k c -> (a p k c)"
            P5 = "e t p k c -> (e t p k c)"
            for l in range(Lr):
                jobs = [(self.b_win[part][l], self.i_win[l, part * 56:(part + 1) * 56], P4, 56 * 128 * KC * 128) for part in range(3)]
                jobs += [
                    (self.b_rgw[l], self.i_rgw[l], "a b i j -> (a b i j)", 48 * 128 * 128),
                    (self.b_poolw[l], self.i_poolw[l], "g c d -> (g c d)", 4 * 384 * 384),
                    (self.b_wbr[l], self.i_wbr[l], P4, 3 * BW * D),
                    (self.b_wout[l], self.i_wout[l], P4, D * D),
                    (self.b_w1[l], self.i_w1[l], P5, NE * D * FF),
                    (self.b_w3[l], self.i_w3[l], P5, NE * D * FF),
                    (self.b_w2[l], self.i_w2[l], P4, NE * FF * D),
                ]
                for dst, src, pat, n in jobs:
                    d2, s2 = flat(dst, pat), flat(src, pat)
                    rows = n // 4096
                    for r0 in range(0, rows, 128):
                        r1 = min(rows, r0 + 128)
                        nr = r1 - r0
                        b = it % 4
                        it += 1
                        self.load(fi[b][0:nr, :], s2[r0:r1, :], "pc_f%d" % b)
                        self.cp(bo[b][0:nr, :], fi[b][0:nr, :], ["pc_f%d" % b], ["pc_b%d" % b], eng=("dve", "act", "dve", "pool")[b])
                        self.store(d2[r0:r1, :], bo[b][0:nr, :], "pc_b%d" % b)
            S.flush()

    def xsrc(self, l):
        return self.d_xs

    def stage_modulate(self, l):
        S = self.S
        src = self.xsrc(l)
        with ExitStack() as es:
            xin = [self.sb(es, "mx%d" % i, [128, T], F32) for i in range(2)]
            uo = [self.sb(es, "mu%d" % i, [128, T], BF16) for i in range(2)]
            def flat(ap, pat):
                return ap.rearrange(pat).rearrange("(r c) -> r c", c=4096)
            for p_ in (range(3) if USE_FORI else ()):
                S.dma("sp", lambda e, p_=p_: e.dma_start(out=flat(self.c_win[p_], "a p k c -> (a p k c)"), in_=flat(self.b_win[p_][l], "a p k c -> (a p k c)")), "wc%d" % p_)
            for i_, (cd, bs, pat) in enumerate(() if not USE_FORI else ((self.c_rgw, self.b_rgw, "a b i j -> (a b i j)"), (self.c_poolw, self.b_poolw, "g c d -> (g c d)"),
                                              (self.c_wbr, self.b_wbr, "a p k c -> (a p k c)"), (self.c_wout, self.b_wout, "a p k c -> (a p k c)"),
                                              (self.c_w1, self.b_w1, "e t p k c -> (e t p k c)"), (self.c_w3, self.b_w3, "e t p k c -> (e t p k c)"),
                                              (self.c_w2, self.b_w2, "a p k c -> (a p k c)"))):
                S.dma("act", lambda e, cd=cd, bs=bs, pat=pat: e.dma_start(out=flat(cd, pat), in_=flat(bs[l], pat)), "wd%d" % i_)
            self.load(self.modLc[:, :, :], self.d_modL[l].rearrange("p (j c) -> p j c", j=NMOD), "modL")
            self.load(self.modCc[:, :, :], self.d_modC[l].rearrange("p (j c) -> p j c", j=NMOD), "modC")
            self.load(self.vecsc[:, :], self.i_vecs[l], "vecs_sb")
            self.load(self.dnormc[:, :], self.d_dnorm[l], "dnorm")
            self.load(self.lamvc[:, :], self.d_lamv[l], "lamv")
            self.load(self.c8c[:, :], self.d_c8[l], "c8")
            for c in range(KC):
                b = c % 2
                self.load(xin[b][:, :], src[c, :, :], "mx%d" % b)
                self.ts(uo[b][:, 0:CTX], xin[b][:, 0:CTX], self.modCc[:, 1, c:c + 1], self.modCc[:, 0, c:c + 1],
                        ALU.mult, ALU.add, ["mx%d" % b, "modC"], ["mu%d" % b])
                self.ts(uo[b][:, CTX:T], xin[b][:, CTX:T], self.modLc[:, 1, c:c + 1], self.modLc[:, 0, c:c + 1],
                        ALU.mult, ALU.add, ["mx%d" % b, "modL"], ["mu%d" % b])
                self.store(self.d_u[c, :, :], uo[b][:, :], "mu%d" % b)
            S.flush()

    def stage_win(self, l):
        S = self.S
        TP = 1280
        with ExitStack() as es:
            ut = self.sb(es, "ut", [128, KC, TP], BF16)
            wb = [self.sb(es, "wb%d" % i, [128, KC, 128], BF16) for i in range(3)]
            of = [self.sb(es, "of%d" % i, [128, 512], F32) for i in range(3)]
            ob = [self.sb(es, "ob%d" % i, [128, 512], BF16) for i in range(3)]
            qf = [self.sb(es, "qf%d" % i, [128, 512], F32) for i in range(2)]
            t1 = [self.sb(es, "t1%d" % i, [128, 512], F32) for i in range(2)]
            rope = self.sb(es, "rope", [128, 2 * T], F32)
            pp = [self.ps(es, "pp%d" % i) for i in range(4)]
            pq = [self.ps(es, "pq%d" % i) for i in range(2)]
            self.load(rope[:, :], self.i_cf[:, 384:384 + 2 * T], "rope")
            nblk = INC // 128
            it = 0
            io = 0
            for (pass_chunks, base, ntok) in ((CHUNKS[0:3], 0, 1280), (CHUNKS[3:5], 1280, 1024)):
                for kc in range(KC):
                    self.load(ut[:, kc, 0:ntok], self.d_u[kc, :, base:base + ntok], "ut")
                for j in range(nblk):
                    w = wb[j % 3]
                    wk = "wb%d" % (j % 3)
                    self.load(w[:, :, :], self.c_win[j // 56][j % 56], wk)
                    sec = j // 12
                    for ci, (t0, tn) in enumerate(pass_chunks):
                        p = pp[it % 4]
                        pk = "pp%d" % (it % 4)
                        it += 1
                        b = io % 3
                        io += 1
                        ofk, obk = "of%d" % b, "ob%d" % b
                        u0 = t0 - base
                        for kc in range(KC):
                            self.mm(p[:, 0:tn], w[:, kc, :], ut[:, kc, u0:u0 + tn], kc == 0, kc == KC - 1, [wk, "ut"], pk)
                        if sec == 0 or sec == 5:
                            self.cp(of[b][:, 0:tn], p[:, 0:tn], [pk], [ofk], eng="act")
                        elif sec == 1:
                            tb = t1[ci % 2]
                            tk = "t1%d" % (ci % 2)
                            self.act(tb[:, 0:tn], p[:, 0:tn], AF.Square, [pk], [tk])
                            self.ts(tb[:, 0:tn], tb[:, 0:tn], 0.044715, 1.0, ALU.mult, ALU.add, [tk], [tk])
                            self.tt(tb[:, 0:tn], tb[:, 0:tn], p[:, 0:tn], ALU.mult, [tk, pk], [tk])
                            self.act(tb[:, 0:tn], tb[:, 0:tn], AF.Sigmoid, [tk], [tk], scale=2.0 * math.sqrt(2.0 / math.pi))
                            self.tt(of[b][:, 0:tn], tb[:, 0:tn], p[:, 0:tn], ALU.mult, [tk, pk], [ofk])
                        elif sec in (2, 3):
                            qb = qf[ci % 2]
                            qk_ = "qf%d" % (ci % 2)
                            tb = t1[ci % 2]
                            tk = "t1%d" % (ci % 2)
                            p2 = pq[ci % 2]
                            p2k = "pq%d" % (ci % 2)
                            self.cp(qb[:, 0:tn], p[:, 0:tn], [pk], [qk_], eng="act")
                            self.mm(p2[:, 0:tn], self.cf[:, 128:256], qb[:, 0:tn], True, True, ["cf_sb", qk_], p2k)
                            self.tt(tb[:, 0:tn], qb[:, 0:tn], rope[:, t0:t0 + tn], ALU.mult, [qk_, "rope"], [tk])
                            self.tt(qb[:, 0:tn], p2[:, 0:tn], rope[:, T + t0:T + t0 + tn], ALU.mult, [p2k, "rope"], [qk_])
                            self.tt(ob[b][:, 0:tn], tb[:, 0:tn], qb[:, 0:tn], ALU.add, [tk, qk_], [obk])
                        elif sec == 4:
                            self.cp(ob[b][:, 0:tn], p[:, 0:tn], [pk], [obk], eng="act")
                        else:
                            self.act(of[b][:, 0:tn], p[:, 0:tn], AF.Sigmoid, [pk], [ofk])
                        if sec == 0:
                            self.store(self.d_xa[j, :, t0:t0 + tn], of[b][:, 0:tn], ofk)
                        elif sec == 1:
                            self.store(self.d_ga[j - 12, :, t0:t0 + tn], of[b][:, 0:tn], ofk)
                        elif sec in (2, 3):
                            self.store(self.d_qk[j - 24, :, t0:t0 + tn], ob[b][:, 0:tn], obk)
                        elif sec == 4:
                            self.store(self.d_v[j - 48, :, t0:t0 + tn], ob[b][:, 0:tn], obk)
                        elif sec == 5:
                            self.store(self.d_pl[j - 60, :, t0:t0 + tn], of[b][:, 0:tn], ofk)
                        else:
                            self.store(self.d_g[j - 72, :, t0:t0 + tn], of[b][:, 0:tn], ofk)
            S.flush()

    def stage_rglru(self, l):
        S = self.S
        vec = self.vecsc
        SEG = [(0, CTX), (CTX, SEQ)]
        with ExitStack() as es:
            xa = self.sb(es, "r_xa", [128, T], F32)
            ga = self.sb(es, "r_ga", [128, T], F32)
            xc = self.sb(es, "r_xc", [128, T], F32)
            xcb = self.sb(es, "r_xcb", [128, T], BF16)
            wg = self.sb(es, "r_wg", [128, 4, 128], BF16)
            av = self.sb(es, "r_a", [128, T], F32)
            bv = self.sb(es, "r_b", [128, T], F32)
            tv = self.sb(es, "r_t", [128, T], F32)
            hv = [self.sb(es, "r_h%d" % i, [128, T], F32) for i in range(2)]
            yo = self.sb(es, "r_y", [128, T], BF16)
            pr = [self.ps(es, "r_p%d" % i) for i in range(4)]
            it = 0
            for cb in range(12):
                self.load(xa[:, :], self.d_xa[cb, :, :], "r_xa")
                self.load(ga[:, :], self.d_ga[cb, :, :], "r_ga")
                self.load(wg[:, :, :], self.c_rgw[:, cb, :, :].rearrange("a i j -> i a j"), "r_wg")
                cw = lambda k: vec[:, V_CW + k * 12 + cb:V_CW + k * 12 + cb + 1]
                for (s0, sn) in SEG:
                    self.ts(xc[:, s0:s0 + sn], xa[:, s0:s0 + sn], cw(1), vec[:, V_CB + cb:V_CB + cb + 1], ALU.mult, ALU.add,
                            ["r_xa", "vecs_sb"], ["r_xc"])
                    self.stt(xc[:, s0 + 1:s0 + sn], xa[:, s0:s0 + sn - 1], cw(0), xc[:, s0 + 1:s0 + sn], ALU.mult, ALU.add,
                             ["r_xa", "r_xc", "vecs_sb"], ["r_xc"])
                    self.stt(xc[:, s0:s0 + sn - 1], xa[:, s0 + 1:s0 + sn], cw(2), xc[:, s0:s0 + sn - 1], ALU.mult, ALU.add,
                             ["r_xa", "r_xc", "vecs_sb"], ["r_xc"])
                    self.stt(xc[:, s0:s0 + sn - 2], xa[:, s0 + 2:s0 + sn], cw(3), xc[:, s0:s0 + sn - 2], ALU.mult, ALU.add,
                             ["r_xa", "r_xc", "vecs_sb"], ["r_xc"])
                self.cp(xcb[:, :], xc[:, :], ["r_xc"], ["r_xcb"], eng="pool")
                for d in range(2):
                    for gi, dst, dk, bias0 in ((0, av, "r_a", V_BA), (1, bv, "r_b", V_BX)):
                        for (t0, tn) in CHUNKS:
                            p = pr[it % 4]
                            pk = "r_p%d" % (it % 4)
                            it += 1
                            self.mm(p[:, 0:tn], wg[:, gi * 2 + d, :], xcb[:, t0:t0 + tn], True, True, ["r_wg", "r_xcb"], pk)
                            self.act(dst[:, t0:t0 + tn], p[:, 0:tn], AF.Sigmoid, [pk, "vecs_sb"], [dk],
                                     bias=vec[:, bias0 + d * 12 + cb:bias0 + d * 12 + cb + 1])
                    self.act(av[:, :], av[:, :], AF.Exp, ["r_a", "c8"], ["r_a"], scale=self.c8c[:, d * 12 + cb:d * 12 + cb + 1])
                    self.tt(tv[:, :], av[:, :], av[:, :], ALU.mult, ["r_a"], ["r_t"])
                    self.ts(tv[:, :], tv[:, :], -1.0, 1.0, ALU.mult, ALU.add, ["r_t"], ["r_t"])
                    self.act(tv[:, :], tv[:, :], AF.Sqrt, ["r_t"], ["r_t"])
                    self.tt(bv[:, :], bv[:, :], xc[:, :], ALU.mult, ["r_b", "r_xc"], ["r_b"])
                    self.tt(bv[:, :], bv[:, :], tv[:, :], ALU.mult, ["r_b", "r_t"], ["r_b"])
                    h = hv[d]
                    hk = "r_h%d" % d
                    if d == 0:
                        S.op("dve", lambda e, h=h: e.tensor_tensor_scan(out=h[:, 0:CTX], data0=av[:, 0:CTX], data1=bv[:, 0:CTX],
                                                                          initial=0.0, op0=ALU.mult, op1=ALU.add),
                             reads=["r_a", "r_b"], writes=[hk])
                        S.op("dve", lambda e, h=h: e.tensor_tensor_scan(out=h[:, CTX:T], data0=av[:, CTX:T], data1=bv[:, CTX:T],
                                                                          initial=h[:, CTX - 1:CTX], op0=ALU.mult, op1=ALU.add),
                             reads=["r_a", "r_b", hk], writes=[hk])
                    else:
                        S.op("dve", lambda e, h=h: e.tensor_tensor_scan(out=h[:, 0:CTX][:, ::-1], data0=av[:, 0:CTX][:, ::-1],
                                                                          data1=bv[:, 0:CTX][:, ::-1],
                                                                          initial=0.0, op0=ALU.mult, op1=ALU.add),
                             reads=["r_a", "r_b"], writes=[hk])
                        S.op("dve", lambda e, h=h: e.tensor_tensor_scan(out=h[:, CTX:T][:, ::-1], data0=av[:, CTX:T][:, ::-1],
                                                                          data1=bv[:, CTX:T][:, ::-1],
                                                                          initial=h[:, 0:1], op0=ALU.mult, op1=ALU.add),
                             reads=["r_a", "r_b", hk], writes=[hk])
                self.tt(tv[:, :], hv[0][:, :], hv[1][:, :], ALU.add, ["r_h0", "r_h1"], ["r_t"])
                self.tt(yo[:, :], tv[:, :], ga[:, :], ALU.mult, ["r_t", "r_ga"], ["r_y"])
                self.store(self.d_y[cb, :, :], yo[:, :], "r_y")
            S.flush()

    def stage_pool(self, l):
        S = self.S
        PADL = 16
        WT = T + 64
        with ExitStack() as es:
            vraw = self.sb(es, "c_v", [128, T], F32)
            bufs = [self.sb(es, "c_s%d" % i, [128, 2, SEQ + 32], F32) for i in range(2)]
            rc = self.sb(es, "c_rc", [128, T], F32)
            pb = self.sb(es, "c_pb", [128, 3, T], BF16)
            pw = self.sb(es, "c_pw", [128, 3, 384], BF16)
            yo = [self.sb(es, "c_y%d" % i, [128, T], BF16) for i in range(2)]
            pr = [self.ps(es, "c_p%d" % i) for i in range(4)]
            for i in range(2):
                S.op("pool", lambda e, i=i: e.memset(bufs[i][:, :, :], 0.0), writes=["c_s%d" % i])
            it = 0
            for g in range(4):
                win = (2, 4, 8, 16)[g]
                self.load(rc[:, :], self.i_rc[g, :, :], "c_rc")
                self.load(pw[:, :, :], self.c_poolw[g, :, :].rearrange("(cc p) d -> p cc d", p=128), "c_pw")
                for bi in range(3):
                    cb = g * 3 + bi
                    self.load(vraw[:, :], self.d_pl[cb, :, :], "c_v")
                    A, Bf = bufs[0], bufs[1]
                    for si, (s0, sn) in enumerate(((0, CTX), (CTX, SEQ))):
                        self.cp(A[:, si, PADL:PADL + sn], vraw[:, s0:s0 + sn], ["c_v"], ["c_s0"], eng="pool")
                    cur, nxt, ck, nk = A, Bf, "c_s0", "c_s1"
                    Lmax = SEQ + 32
                    w = 1
                    while w < win:
                        if w == 1:
                            lo, hi = 1, Lmax
                            self.tt(nxt[:, :, lo:hi], cur[:, :, lo - 1:hi - 1], cur[:, :, lo:hi], ALU.add, [ck], [nk])
                        else:
                            hsh = w // 2
                            lo, hi = w, Lmax - w
                            self.tt(nxt[:, :, lo:hi], cur[:, :, lo - hsh:hi - hsh], cur[:, :, lo + hsh:hi + hsh], ALU.add, [ck], [nk])
                        cur, nxt, ck, nk = nxt, cur, nk, ck
                        w *= 2
                    for si, (s0, sn) in enumerate(((0, CTX), (CTX, SEQ))):
                        self.tt(cur[:, si, PADL:PADL + sn], cur[:, si, PADL:PADL + sn], rc[:, s0:s0 + sn], ALU.mult, [ck, "c_rc"], [ck])
                        self.tt(pb[:, bi, s0:s0 + sn], cur[:, si, PADL:PADL + sn], vraw[:, s0:s0 + sn], ALU.subtract,
                                [ck, "c_v"], ["c_pb"])
                    for i in range(2):
                        S.op("pool", lambda e, i=i: e.memset(bufs[i][:, :, :], 0.0), writes=["c_s%d" % i])
                for dtile in range(3):
                    b = dtile % 2
                    for (t0, tn) in CHUNKS:
                        p = pr[it % 4]
                        pk = "c_p%d" % (it % 4)
                        it += 1
                        for cc in range(3):
                            self.mm(p[:, 0:tn], pw[:, cc, dtile * 128:(dtile + 1) * 128], pb[:, cc, t0:t0 + tn],
                                    cc == 0, cc == 2, ["c_pw", "c_pb"], pk)
                        cbo = g * 3 + dtile
                        self.ts(yo[b][:, t0:t0 + tn], p[:, 0:tn], self.vecsc[:, V_PS + cbo:V_PS + cbo + 1], None, ALU.mult, None,
                                [pk, "vecs_sb"], ["c_y%d" % b])
                    self.store(self.d_y[24 + g * 3 + dtile, :, :], yo[b][:, :], "c_y%d" % b)
            S.flush()

    def stage_attn(self, l):
        S = self.S
        sc = 1.0 / math.sqrt(128.0)
        NT = T // 128
        with ExitStack() as es:
            qk = self.sb(es, "a_qk", [128, 4, T], BF16)
            vT = self.sb(es, "a_vT", [128, 2, T], BF16)
            vtm = self.sb(es, "a_vtm", [128, NT, 264], BF16)
            et = [self.sb(es, "a_e%d" % i, [128, 512], BF16) for i in range(3)]
            o0 = [self.sb(es, "a_o%d" % i, [128, 264], F32) for i in range(4)]
            of = self.sb(es, "a_of", [128, 256], F32)
            sq = self.sb(es, "a_sq", [128, 256], F32)
            sm = self.sb(es, "a_sm", [128, 8], F32)
            ob = self.sb(es, "a_ob", [128, 256], BF16)
            yb = self.sb(es, "a_yb", [128, 2, T], BF16)
            pst = [self.ps(es, "a_ps%d" % i) for i in range(2)]
            pov = [self.ps(es, "a_po%d" % i) for i in range(4)]
            ptr = [self.ps(es, "a_pt%d" % i, [128, 512], BF16) for i in range(2)]
            S.op("pool", lambda e: e.memset(vtm[:, :, :], 1.0), writes=["a_vtm"])
            ie = 0
            itr = 0
            for h in range(6):
                for c in range(2):
                    self.load(qk[:, c, :], self.d_qk[h * 2 + c, :, :], "a_qk")
                    self.load(qk[:, 2 + c, :], self.d_qk[12 + h * 2 + c, :, :], "a_qk")
                    self.load(vT[:, c, :], self.d_v[h * 2 + c, :, :], "a_vT")
                for tt_ in range(NT):
                    pt = ptr[itr % 2]
                    ptk = "a_pt%d" % (itr % 2)
                    itr += 1
                    for eb in range(2):
                        S.op("pe", lambda e, pt=pt, eb=eb, tt_=tt_: e.transpose(out=pt[:, eb * 128:(eb + 1) * 128],
                                                                                in_=vT[:, eb, tt_ * 128:(tt_ + 1) * 128],
                                                                                identity=self.cb[:, :]),
                             reads=["a_vT", "cb_sb"], writes=[ptk])
                    self.cp(vtm[:, tt_, 0:256], pt[:, 0:256], [ptk], ["a_vtm"], eng="act")
                for ci, (t0, tn) in enumerate(CHUNKS):
                    nk = 2 if ci == 0 else NT
                    nq = tn // 128
                    for c in range(2):
                        for kt in range(nk):
                            p = pst[ie % 2]
                            pk = "a_ps%d" % (ie % 2)
                            e_ = et[ie % 3]
                            ek = "a_e%d" % (ie % 3)
                            ie += 1
                            self.mm(p[:, 0:tn], qk[:, 2 + c, kt * 128:(kt + 1) * 128], qk[:, c, t0:t0 + tn], True, True, ["a_qk"], pk)
                            self.act(e_[:, 0:tn], p[:, 0:tn], AF.Exp, [pk], [ek], scale=sc)
                            for qi in range(nq):
                                self.mm(pov[qi][:, 0:257], e_[:, qi * 128:(qi + 1) * 128], vtm[:, kt, 0:257],
                                        kt == 0, kt == nk - 1, [ek, "a_vtm"], "a_po%d" % qi)
                        if c == 0:
                            for qi in range(nq):
                                self.cp(o0[qi][:, 0:257], pov[qi][:, 0:257], ["a_po%d" % qi], ["a_o%d" % qi], eng="act")
                        else:
                            for qi in range(nq):
                                ok = "a_o%d" % qi
                                pok = "a_po%d" % qi
                                S.op("dve", lambda e, qi=qi: e.reciprocal(out=sm[:, 0:1], in_=o0[qi][:, 256:257]),
                                     reads=[ok], writes=["a_sm"])
                                S.op("dve", lambda e, qi=qi: e.reciprocal(out=sm[:, 1:2], in_=pov[qi][:, 256:257]),
                                     reads=[pok], writes=["a_sm"])
                                self.tt(sm[:, 1:2], sm[:, 1:2], self.lamvc[:, 1:2], ALU.mult, ["a_sm", "lamv"], ["a_sm"])
                                self.ts(of[:, :], o0[qi][:, 0:256], sm[:, 0:1], None, ALU.mult, None, [ok, "a_sm"], ["a_of"])
                                self.stt(of[:, :], pov[qi][:, 0:256], sm[:, 1:2], of[:, :], ALU.mult, ALU.add, [pok, "a_sm", "a_of"], ["a_of"])
                                self.act(sq[:, :], of[:, :], AF.Square, ["a_of"], ["a_sq"])
                                S.op("dve", lambda e: e.reduce_sum(out=sm[:, 2:3], in_=sq[:, :], axis=mybir.AxisListType.X),
                                     reads=["a_sq"], writes=["a_sm"])
                                self.ts(sm[:, 2:3], sm[:, 2:3], 1.0 / 256.0, EPS, ALU.mult, ALU.add, ["a_sm"], ["a_sm"])
                                self.act(sm[:, 3:4], sm[:, 2:3], AF.Sqrt, ["a_sm"], ["a_sm"])
                                S.op("dve", lambda e: e.reciprocal(out=sm[:, 4:5], in_=sm[:, 3:4]), reads=["a_sm"], writes=["a_sm"])
                                self.stt(ob[:, :], of[:, :], sm[:, 4:5], self.dnormc[:, :], ALU.mult, ALU.mult,
                                         ["a_of", "a_sm", "dnorm"], ["a_ob"])
                                pt = ptr[itr % 2]
                                ptk = "a_pt%d" % (itr % 2)
                                itr += 1
                                for eb in range(2):
                                    S.op("pe", lambda e, pt=pt, eb=eb: e.transpose(out=pt[:, eb * 128:(eb + 1) * 128],
                                                                                   in_=ob[:, eb * 128:(eb + 1) * 128],
                                                                                   identity=self.cb[:, :]),
                                         reads=["a_ob", "cb_sb"], writes=[ptk])
                                q0 = t0 + qi * 128
                                for eb in range(2):
                                    self.cp(yb[:, eb, q0:q0 + 128], pt[:, eb * 128:(eb + 1) * 128], [ptk], ["a_yb"], eng="act")
                for eb in range(2):
                    self.store(self.d_y[12 + h * 2 + eb, :, :], yb[:, eb, :], "a_yb")
            S.flush()

    def stage_merge(self, l, last):
        S = self.S
        chunks = CHUNKS[1:] if last else CHUNKS
        with ExitStack() as es:
            yk = self.sb(es, "m_y", [128, 36, 512], BF16)
            wt = [self.sb(es, "m_w%d" % i, [128, 36, 128], BF16) for i in range(2)]
            gt = [self.sb(es, "m_g%d" % i, [128, 3, 512], F32) for i in range(2)]
            acc = [self.sb(es, "m_a%d" % i, [128, 512], F32) for i in range(2)]
            tmp = [self.sb(es, "m_t%d" % i, [128, 512], F32) for i in range(2)]
            mo = [self.sb(es, "m_o%d" % i, [128, 512], BF16) for i in range(2)]
            pm = [self.ps(es, "m_p%d" % i) for i in range(6)]
            it = 0
            for (t0, tn) in chunks:
                for cbk in range(36):
                    self.load(yk[:, cbk, 0:tn], self.d_y[cbk, :, t0:t0 + tn], "m_y")
                for dt_ in range(KC):
                    b = dt_ % 2
                    self.load(wt[b][:, :, :], self.c_wbr[dt_], "m_w%d" % b)
                    for k in range(3):
                        self.load(gt[b][:, k, 0:tn], self.d_g[k * KC + dt_, :, t0:t0 + tn], "m_g%d" % b)
                    for k in range(3):
                        p = pm[it % 6]
                        pk = "m_p%d" % (it % 6)
                        it += 1
                        for cc in range(12):
                            self.mm(p[:, 0:tn], wt[b][:, k * 12 + cc, :], yk[:, k * 12 + cc, 0:tn], cc == 0, cc == 11,
                                    ["m_w%d" % b, "m_y"], pk)
                        if k == 0:
                            self.tt(acc[b][:, 0:tn], p[:, 0:tn], gt[b][:, 0, 0:tn], ALU.mult, [pk, "m_g%d" % b], ["m_a%d" % b])
                        else:
                            self.tt(tmp[b][:, 0:tn], p[:, 0:tn], gt[b][:, k, 0:tn], ALU.mult, [pk, "m_g%d" % b], ["m_t%d" % b])
                            if k == 1:
                                self.tt(acc[b][:, 0:tn], acc[b][:, 0:tn], tmp[b][:, 0:tn], ALU.add, ["m_a%d" % b, "m_t%d" % b],
                                        ["m_a%d" % b], eng="pool")
                            else:
                                self.tt(mo[b][:, 0:tn], acc[b][:, 0:tn], tmp[b][:, 0:tn], ALU.add, ["m_a%d" % b, "m_t%d" % b],
                                        ["m_o%d" % b], eng="pool")
                    self.store(self.d_m[dt_, :, t0:t0 + tn], mo[b][:, 0:tn], "m_o%d" % b)
            S.flush()

    def stage_proj_ln(self, l, last, which):
        S = self.S
        chunks = CHUNKS[1:] if last else CHUNKS
        if which == 2:
            chunks = [(t0 + i * 256, 256) for (t0, tn) in chunks for i in range(tn // 256)]
        CW = 512 if which == 1 else 256
        gidx = 2 if which == 1 else 5
        vg, vb = (V_LN1G, V_LN1B) if which == 1 else (V_LN2G, V_LN2B)
        src = self.xsrc(l) if which == 1 else self.d_xs
        with ExitStack() as es:
            xin = self.sb(es, "l_in", [128, KC, CW], BF16)
            rt = self.sb(es, "l_r", [128, KC, CW], F32)
            xt = [self.sb(es, "l_x%d" % i, [128, CW], F32) for i in range(2)]
            tq = [self.sb(es, "l_t%d" % i, [128, CW], F32) for i in range(2)]
            sq = [self.sb(es, "l_q%d" % i, [128, CW], F32) for i in range(2)]
            mean = self.sb(es, "l_mean", [128, CW], F32)
            rstd = self.sb(es, "l_rstd", [128, CW], F32)
            xo = [self.sb(es, "l_xo%d" % i, [128, CW], F32) for i in range(2)]
            uo = [self.sb(es, "l_uo%d" % i, [128, CW], F32) for i in range(2)]
            ub = [self.sb(es, "l_ub%d" % i, [128, CW], BF16) for i in range(2)]
            pm = [self.ps(es, "l_p%d" % i) for i in range(2)]
            pss = self.ps(es, "l_ps")
            psq = self.ps(es, "l_pq")
            if which == 1:
                wt = [self.sb(es, "l_w%d" % i, [128, KC, 128], BF16) for i in range(2)]
            else:
                wt = [self.sb(es, "l_w%d" % i, [128, 48, 128], BF16) for i in range(2)]
                w13 = [self.sb(es, "l_e%d" % i, [128, 2, KC, 128], BF16) for i in range(2)]
                hT = self.sb(es, "l_h", [128, 48, CW], BF16)
                gsb = self.sb(es, "l_G", [128, CW], F32)
                gbs = [self.sb(es, "l_gb%d" % i, [128, CW], F32) for i in range(2)]
                s1 = [self.sb(es, "l_s%d" % i, [128, CW], F32) for i in range(2)]
                ph = [self.ps(es, "l_ph%d" % i) for i in range(2)]
                pg = self.ps(es, "l_pg")
            it = 0
            if which == 2:
                S.op("pool", lambda e: e.memset(gsb[:, :], 0.0), writes=["l_G"])
            for ci, (t0, tn) in enumerate(chunks):
                mod = self.modCc if t0 < CTX else self.modLc
                mk = "modC" if t0 < CTX else "modL"
                if which == 1:
                    for kc in range(KC):
                        self.load(xin[:, kc, 0:tn], self.d_m[kc, :, t0:t0 + tn], "l_in")
                else:
                    for kc in range(KC):
                        self.load(xin[:, kc, 0:tn], self.d_u2[kc, :, t0:t0 + tn], "l_in")
                    self.load(gsb[0:16, 0:tn], self.d_G[:, t0:t0 + tn], "l_G")
                    ih = 0
                    for e_ in range(NE):
                        gb = gbs[e_ % 2]
                        gk = "l_gb%d" % (e_ % 2)
                        self.mm(pg[:, 0:tn], self.sel[:, e_ * 128:(e_ + 1) * 128], gsb[:, 0:tn], True, True, ["sel_sb", "l_G"], "l_pg")
                        self.cp(gb[:, 0:tn], pg[:, 0:tn], ["l_pg"], [gk], eng="act")
                        for ft in range(3):
                            wb_ = w13[ih % 2]
                            wk = "l_e%d" % (ih % 2)
                            ih += 1
                            self.load(wb_[:, 0, :, :], self.c_w1[e_, ft], wk)
                            self.load(wb_[:, 1, :, :], self.c_w3[e_, ft], wk)
                            for wi in range(2):
                                for kc in range(KC):
                                    self.mm(ph[wi][:, 0:tn], wb_[:, wi, kc, :], xin[:, kc, 0:tn], kc == 0, kc == KC - 1,
                                            [wk, "l_in"], "l_ph%d" % wi)
                            sb_ = s1[ih % 2]
                            sk = "l_s%d" % (ih % 2)
                            self.act(sb_[:, 0:tn], ph[0][:, 0:tn], AF.Silu, ["l_ph0"], [sk])
                            self.tt(sb_[:, 0:tn], sb_[:, 0:tn], ph[1][:, 0:tn], ALU.mult, [sk, "l_ph1"], [sk])
                            self.tt(hT[:, e_ * 3 + ft, 0:tn], sb_[:, 0:tn], gb[:, 0:tn], ALU.mult, [sk, gk], ["l_h"], eng="pool")
                for dt_ in range(KC):
                    b = dt_ % 2
                    p = pm[b]
                    pk = "l_p%d" % b
                    if which == 1:
                        self.load(wt[b][:, :, :], self.c_wout[dt_], "l_w%d" % b)
                        for kc in range(KC):
                            self.mm(p[:, 0:tn], wt[b][:, kc, :], xin[:, kc, 0:tn], kc == 0, kc == KC - 1, ["l_w%d" % b, "l_in"], pk)
                    else:
                        self.load(wt[b][:, :, :], self.c_w2[dt_], "l_w%d" % b)
                        for ef in range(48):
                            self.mm(p[:, 0:tn], wt[b][:, ef, :], hT[:, ef, 0:tn], ef == 0, ef == 47, ["l_w%d" % b, "l_h"], pk)
                    self.load(xt[b][:, 0:tn], src[dt_, :, t0:t0 + tn], "l_x%d" % b)
                    self.ts(tq[b][:, 0:tn], p[:, 0:tn], mod[:, gidx, dt_:dt_ + 1], None, ALU.mult, None, [pk, mk], ["l_t%d" % b])
                    self.stt(rt[:, dt_, 0:tn], xt[b][:, 0:tn], float(ALPHA), tq[b][:, 0:tn], ALU.mult, ALU.add,
                             ["l_x%d" % b, "l_t%d" % b], ["l_r"])
                    self.act(sq[b][:, 0:tn], rt[:, dt_, 0:tn], AF.Square, ["l_r"], ["l_q%d" % b])
                    self.mm(pss[:, 0:tn], self.cf[:, 256:384], rt[:, dt_, 0:tn], dt_ == 0, dt_ == KC - 1, ["cf_sb", "l_r"], "l_ps")
                    self.mm(psq[:, 0:tn], self.cf[:, 256:384], sq[b][:, 0:tn], dt_ == 0, dt_ == KC - 1, ["cf_sb", "l_q%d" % b], "l_pq")
                self.ts(mean[:, 0:tn], pss[:, 0:tn], 1.0 / D, None, ALU.mult, None, ["l_ps"], ["l_mean"])
                self.tt(rstd[:, 0:tn], mean[:, 0:tn], mean[:, 0:tn], ALU.mult, ["l_mean"], ["l_rstd"])
                self.stt(rstd[:, 0:tn], psq[:, 0:tn], 1.0 / D, rstd[:, 0:tn], ALU.mult, ALU.subtract, ["l_pq", "l_rstd"], ["l_rstd"])
                self.ts(rstd[:, 0:tn], rstd[:, 0:tn], EPS, None, ALU.add, None, ["l_rstd"], ["l_rstd"])
                self.act(rstd[:, 0:tn], rstd[:, 0:tn], AF.Sqrt, ["l_rstd"], ["l_rstd"])
                S.op("dve", lambda e, tn=tn: e.reciprocal(out=rstd[:, 0:tn], in_=rstd[:, 0:tn]), reads=["l_rstd"], writes=["l_rstd"])
                for dt_ in range(KC):
                    b = dt_ % 2
                    self.tt(tq[b][:, 0:tn], rt[:, dt_, 0:tn], mean[:, 0:tn], ALU.subtract, ["l_r", "l_mean"], ["l_t%d" % b])
                    self.tt(tq[b][:, 0:tn], tq[b][:, 0:tn], rstd[:, 0:tn], ALU.mult, ["l_t%d" % b, "l_rstd"], ["l_t%d" % b], eng="pool")
                    self.ts(xo[b][:, 0:tn], tq[b][:, 0:tn], self.vecsc[:, vg + dt_:vg + dt_ + 1], self.vecsc[:, vb + dt_:vb + dt_ + 1],
                            ALU.mult, ALU.add, ["l_t%d" % b, "vecs_sb"], ["l_xo%d" % b])
                    final = False
                    if final and t0 >= CTX:
                        self.store(self.o_out[dt_, :, t0 - CTX:t0 - CTX + tn], xo[b][:, 0:tn], "l_xo%d" % b)
                    else:
                        self.store(self.d_xs[dt_, :, t0:t0 + tn], xo[b][:, 0:tn], "l_xo%d" % b)
                    if which == 1:
                        self.ts(uo[b][:, 0:tn], xo[b][:, 0:tn], mod[:, 4, dt_:dt_ + 1], mod[:, 3, dt_:dt_ + 1], ALU.mult, ALU.add,
                                ["l_xo%d" % b, mk], ["l_uo%d" % b])
                        self.cp(ub[b][:, 0:tn], uo[b][:, 0:tn], ["l_uo%d" % b], ["l_ub%d" % b], eng="act")
                        self.store(self.d_u2f[dt_, :, t0:t0 + tn], uo[b][:, 0:tn], "l_uo%d" % b)
                        self.store(self.d_u2[dt_, :, t0:t0 + tn], ub[b][:, 0:tn], "l_ub%d" % b)
            S.flush()

    def stage_router(self, l, last):
        S = self.S
        import os
        RD = os.environ.get("KDBG", "")
        segs = [(CTX, SEQ, 256)] if last else [(0, CTX, 32), (CTX, SEQ, 256)]
        chunks = CHUNKS[1:] if last else CHUNKS
        with ExitStack() as es:
            rw = self.sb(es, "t_rw", [128, KC, 128], F32)
            uf = [self.sb(es, "t_u%d" % i, [128, 512], F32) for i in range(3)]
            aff = self.sb(es, "t_aff", [128, T], F32)
            wk_ = self.sb(es, "t_wk", [16, T], F32)
            rs = self.sb(es, "t_rs", [16, 512], F32)
            m8 = self.sb(es, "t_m8", [16, 8], F32)
            G = self.sb(es, "t_G", [16, T], F32)
            pl_ = self.ps(es, "t_pl")
            psm = self.ps(es, "t_psm")
            self.load(rw[:, :, :], self.i_router[l].rearrange("(kc p) e -> p kc e", p=128), "t_rw")
            S.op("pool", lambda e: e.memset(G[:, :], 0.0), writes=["t_G"])
            S.op("pool", lambda e: e.memset(aff[:, :], 0.0), writes=["t_aff"])
            iu = 0
            for (t0, tn) in chunks:
                for kc in range(KC):
                    u_ = uf[iu % 3]
                    uk = "t_u%d" % (iu % 3)
                    iu += 1
                    self.load(u_[:, 0:tn], self.d_u2f[kc, :, t0:t0 + tn], uk)
                    self.mm(pl_[:, 0:tn], rw[:, kc, :], u_[:, 0:tn], kc == 0, kc == KC - 1, ["t_rw", uk], "t_pl", inc=True)
                self.act(aff[0:16, t0:t0 + tn], pl_[0:16, 0:tn], AF.Exp, ["t_pl"], ["t_aff"])
                if "R1" in RD:
                    continue
                self.mm(psm[:, 0:tn], self.cf[:, 256:384], aff[:, t0:t0 + tn], True, True, ["cf_sb", "t_aff"], "t_psm")
                S.op("dve", lambda e, tn=tn: e.reciprocal(out=rs[:, 0:tn], in_=psm[0:16, 0:tn]), reads=["t_psm"], writes=["t_rs"])
                self.tt(aff[0:16, t0:t0 + tn], aff[0:16, t0:t0 + tn], rs[:, 0:tn], ALU.mult, ["t_aff", "t_rs"], ["t_aff"])
            for (s0, sn, cap) in segs:
                if "R1" in RD or "R2" in RD:
                    break
                self.cp(wk_[:, s0:s0 + sn], aff[0:16, s0:s0 + sn], ["t_aff"], ["t_wk"])
                for r in range(cap // 8):
                    S.op("dve", lambda e, s0=s0, sn=sn: e.max(out=m8[:, :], in_=wk_[:, s0:s0 + sn]), reads=["t_wk"], writes=["t_m8"])
                    if r < cap // 8 - 1:
                        S.op("dve", lambda e, s0=s0, sn=sn: e.match_replace(out=wk_[:, s0:s0 + sn], in_to_replace=m8[:, :],
                                                                             in_values=wk_[:, s0:s0 + sn], imm_value=-1.0),
                             reads=["t_wk", "t_m8"], writes=["t_wk"])
                self.ts(wk_[:, s0:s0 + sn], aff[0:16, s0:s0 + sn], m8[:, 7:8], None, ALU.is_ge, None, ["t_aff", "t_m8"], ["t_wk"])
                self.tt(G[:, s0:s0 + sn], wk_[:, s0:s0 + sn], aff[0:16, s0:s0 + sn], ALU.mult, ["t_wk", "t_aff"], ["t_G"])
            self.store(self.d_G[:, :], G[:, :], "t_G")
            S.flush()

    def stage_final(self):
        S = self.S
        for i in range(0, KC, 4):
            S.dma("sp", lambda e, i=i: e.dma_start(out=self.o_out[i:i + 4, :, :], in_=self.d_xs[i:i + 4, :, CTX:T]), "fin%d" % (i % 8))
        S.flush()

    def stage_output(self):
        if not self.dump:
            return
        S = self.S
        for name in self.dump:
            src = getattr(self, name)
            shp = list(src.shape)
            o = self.nc.dram_tensor("dump_" + name, shp, src.dtype, kind="ExternalOutput")
            n0 = shp[0]
            step = max(1, n0 // 8)
            for i in range(0, n0, step):
                j = min(n0, i + step)
                S.dma("sp", lambda e, o=o, src=src, i=i, j=j: e.dma_start(out=o[i:j], in_=src[i:j]), "dump")
        S.flush()


def _consts():
    ident = np.eye(128, dtype=np.float32)
    perm = np.zeros((128, 128), np.float32)
    for m in range(128):
        partner = m + 32 if (m % 64) < 32 else m - 32
        perm[partner, m] = 1.0
    ones = np.ones((128, 128), np.float32)
    quarter = 32
    inv_freq = (10000.0 ** (-np.arange(quarter, dtype=np.float32) / quarter)).astype(np.float32)
    n = np.arange(SEQ)
    rows = (n // 64).astype(np.float32)
    cols = (n % 64).astype(np.float32)
    C = np.ones((128, T), np.float32)
    Sg = np.zeros((128, T), np.float32)
    for d in range(128):
        pos = rows if d < 64 else cols
        dd = d % 64
        ang = (pos * inv_freq[dd % 32]).astype(np.float32)
        C[d, CTX:] = np.cos(ang)
        s = np.sin(ang)
        Sg[d, CTX:] = -s if dd < 32 else s
    cf = np.concatenate([ident, perm, ones, C, Sg], axis=1).astype(np.float32)
    cb = np.eye(128, dtype=np.float32).astype(ml_dtypes.bfloat16)
    sel = np.zeros((128, NE * 128), np.float32)
    for e in range(NE):
        sel[e, e * 128:(e + 1) * 128] = 1.0
    rc = np.zeros((4, T), np.float32)
    for g, win in enumerate((2, 4, 8, 16)):
        for (s0, Ls) in ((0, CTX), (CTX, SEQ)):
            t = np.arange(Ls)
            lo = np.clip(t - win // 2, 0, Ls)
            hi = np.clip(t + win // 2, 0, Ls)
            rc[g, s0:s0 + Ls] = 1.0 / (hi - lo).astype(np.float32)
    rc = np.ascontiguousarray(np.broadcast_to(rc[:, None, :], (4, 128, T)))
    return cf, cb, sel, rc


def _pvec(v):
    v = np.asarray(v, np.float32)
    return v.reshape(-1, 128).T


def prepare_inputs(inp, depth, cores):
    f = lambda a: np.ascontiguousarray(np.asarray(a, dtype=np.float32))
    Lr = depth
    cf, cb, sel, rc = _consts()
    vecs = np.zeros((Lr, 128, NV), np.float32)
    dav = np.zeros((Lr, 128, 768), np.float32)
    mbt = np.zeros((Lr, 128, NMOD * KC), np.float32)
    for l in range(Lr):
        vecs[l, :, V_LN1G:V_LN1G + 32] = _pvec(inp["ln1_g"][l])
        vecs[l, :, V_LN1B:V_LN1B + 32] = _pvec(inp["ln1_b"][l])
        vecs[l, :, V_LN2G:V_LN2G + 32] = _pvec(inp["ln2_g"][l])
        vecs[l, :, V_LN2B:V_LN2B + 32] = _pvec(inp["ln2_b"][l])
        for k in range(4):
            vecs[l, :, V_CW + k * 12:V_CW + (k + 1) * 12] = _pvec(inp["conv_w"][l, k])
        vecs[l, :, V_CB:V_CB + 12] = _pvec(inp["conv_b"][l])
        for d in range(2):
            vecs[l, :, V_BA + d * 12:V_BA + (d + 1) * 12] = _pvec(inp["rg_ba"][l, d])
            vecs[l, :, V_BX + d * 12:V_BX + (d + 1) * 12] = _pvec(inp["rg_bx"][l, d])
            vecs[l, :, V_LAM + d * 12:V_LAM + (d + 1) * 12] = _pvec(inp["rg_lambda"][l, d])
        vecs[l, :, V_PS:V_PS + 12] = _pvec(inp["pool_scale"][l])
        for i, nm in enumerate(("da_lq1", "da_lk1", "da_lq2", "da_lk2")):
            dav[l, :, i * 128:(i + 1) * 128] = np.asarray(inp[nm][l], np.float32)[None, :]
        dav[l, :, 512:768] = np.asarray(inp["da_norm"][l], np.float32)[None, :]
        mbt[l] = _pvec(inp["mod_bias"][l])
    rgw = np.ascontiguousarray(np.concatenate([np.asarray(inp["rg_wa"][:Lr], np.float32), np.asarray(inp["rg_wx"][:Lr], np.float32)], axis=1))
    shared = {
        "mod_a": f(inp["mod_a"][:Lr]), "mod_b": f(inp["mod_b"][:Lr]), "mod_bias_t": mbt, "vecs": vecs, "dav": dav,
        "w_in": np.ascontiguousarray(f(inp["w_in"][:Lr]).reshape(Lr, KC, 128, INC // 128, 128).transpose(0, 3, 2, 1, 4)), "rg_w": rgw, "pool_w": f(inp["pool_w"][:Lr]),
        "w_branch": np.ascontiguousarray(f(inp["w_branch"][:Lr]).reshape(Lr, 36, 128, KC, 128).transpose(0, 3, 2, 1, 4)),
        "w_out": np.ascontiguousarray(f(inp["w_out"][:Lr]).reshape(Lr, KC, 128, KC, 128).transpose(0, 3, 2, 1, 4)),
        "router": np.ascontiguousarray(np.pad(f(inp["router"][:Lr]), ((0, 0), (0, 0), (0, 128 - NE)))), "ex_w1": np.ascontiguousarray(f(inp["ex_w1"][:Lr]).reshape(Lr, NE, KC, 128, 3, 128).transpose(0, 1, 4, 3, 2, 5)),
        "ex_w3": np.ascontiguousarray(f(inp["ex_w3"][:Lr]).reshape(Lr, NE, KC, 128, 3, 128).transpose(0, 1, 4, 3, 2, 5)),
        "ex_w2": np.ascontiguousarray(f(inp["ex_w2"][:Lr]).reshape(Lr, 48, 128, KC, 128).transpose(0, 3, 2, 1, 4)),
        "const_f": cf, "const_b": cb, "const_sel": sel, "const_rc": rc,
    }
    maps = []
    cctx = np.asarray(inp["c_ctx"], np.float32)
    for b in cores:
        xT = np.concatenate([np.asarray(inp["ctx"][b], np.float32).T, np.asarray(inp["x"][b], np.float32).T], axis=1)
        cs = np.stack([_pvec(inp["c"][b]), _pvec(cctx)], axis=-1)
        m = dict(shared)
        m["xT"] = np.ascontiguousarray(xT.reshape(KC, 128, T))
        m["cs"] = np.ascontiguousarray(cs)
        maps.append(m)
    return maps


def kernel(**inputs):
    prog = K(L_FULL)
    nc = prog.build()
    maps = prepare_inputs(inputs, L_FULL, list(range(NB)))
    res = run_bass_kernel_spmd(nc, maps, core_ids=list(range(NB)))
    out = np.stack([np.asarray(r["outT"], np.float32).reshape(D, SEQ).T for r in res.results], axis=0)
    return np.ascontiguousarray(out)
```
